# Optimizing a Trainium2 kernel written in Bass

```python
import jax, jax.numpy as jnp
from jax import lax
import numpy as np

D_MODEL = 1024
BATCH = 16
SEQ = 2048
DEPTH = 1
DEC_BATCH = 2
DEC_SEQ = 8192
PAST_LEN = 128

HEAD_DIM = 64
N_Q_HEADS = 8
N_KV_HEADS = 2
GQA_GROUP = N_Q_HEADS // N_KV_HEADS
ATTN_WIDTH = N_Q_HEADS * HEAD_DIM
KV_WIDTH = N_KV_HEADS * HEAD_DIM
WINDOW = 128
BLOCK = 128
ROPE_THETA = 500000.0
ROT_DIM = HEAD_DIM // 4
N_FOURIER_GROUPS = 4
FOURIER_GROUP_WIDTH = 128
FOURIER_WIDTH = N_FOURIER_GROUPS * FOURIER_GROUP_WIDTH
N_BRANCHES = 2
IN_WIDTH = ATTN_WIDTH + 2 * KV_WIDTH + FOURIER_WIDTH + N_BRANCHES * D_MODEL
SPLITS = [ATTN_WIDTH, ATTN_WIDTH + KV_WIDTH, ATTN_WIDTH + 2 * KV_WIDTH,
          ATTN_WIDTH + 2 * KV_WIDTH + FOURIER_WIDTH]
D_FF = 4 * D_MODEL
N_META = 16
META_PAD = BLOCK - N_META
RMS_EPS = 1e-6
NEG_INF = -1e30

kernel_name = "hybrid_gated_swa_fnet_encoder"


def rms_norm(x, g):
    xf = x.astype(jnp.float32)
    y = xf * lax.rsqrt(jnp.mean(xf * xf, axis=-1, keepdims=True) + RMS_EPS)
    return (y * g.astype(jnp.float32)).astype(x.dtype)


def rope_partial(x, pos):
    half = ROT_DIM // 2
    inv_freq = ROPE_THETA ** (-jnp.arange(half, dtype=jnp.float32) / half)
    ang = pos[:, None] * inv_freq[None, :]
    cos = jnp.cos(ang)[None, :, None, :].astype(x.dtype)
    sin = jnp.sin(ang)[None, :, None, :].astype(x.dtype)
    x1, x2, rest = x[..., :half], x[..., half:ROT_DIM], x[..., ROT_DIM:]
    return jnp.concatenate([x1 * cos - x2 * sin, x2 * cos + x1 * sin, rest], axis=-1)


def _band(a, nb, axis):
    return jnp.concatenate([lax.slice_in_dim(a, i, i + nb, axis=axis) for i in range(3)], axis=axis + 1)


def window_attention(q, k, v, sink):
    B, L = q.shape[0], q.shape[1]
    T = META_PAD + L
    nb = T // BLOCK
    scale = HEAD_DIM ** -0.5
    k_meta, v_meta = k[:, :N_META], v[:, :N_META]
    qb = jnp.pad(q, ((0, 0), (META_PAD, 0), (0, 0), (0, 0))).reshape(
        B, nb, BLOCK, N_KV_HEADS, GQA_GROUP, HEAD_DIM)
    kv_pad = ((0, 0), (META_PAD + BLOCK, BLOCK), (0, 0), (0, 0))
    kb = _band(jnp.pad(k, kv_pad).reshape(B, nb + 2, BLOCK, N_KV_HEADS, HEAD_DIM), nb, 1)
    vb = _band(jnp.pad(v, kv_pad).reshape(B, nb + 2, BLOCK, N_KV_HEADS, HEAD_DIM), nb, 1)
    pos_q = (jnp.arange(T) - META_PAD).reshape(nb, BLOCK)
    pos_k = _band((jnp.arange(T + 2 * BLOCK) - META_PAD - BLOCK).reshape(nb + 2, BLOCK), nb, 0)
    key_real = (pos_k >= N_META) & (pos_k < L)
    mask = (jnp.abs(pos_q[:, :, None] - pos_k[:, None, :]) <= WINDOW) & key_real[:, None, :]
    s_band = jnp.einsum('bnqhgd,bnkhd->bnhgqk', qb, kb, preferred_element_type=jnp.float32) * scale
    s_band = jnp.where(mask[None, :, None, None], s_band, NEG_INF)
    s_meta = jnp.einsum('bnqhgd,bmhd->bnhgqm', qb, k_meta, preferred_element_type=jnp.float32) * scale
    sink_l = jnp.broadcast_to(sink.astype(jnp.float32).reshape(1, 1, N_KV_HEADS, GQA_GROUP, 1, 1),
                              s_meta.shape[:-1] + (1,))
    p = jax.nn.softmax(jnp.concatenate([sink_l, s_meta, s_band], axis=-1), axis=-1)
    p_meta = p[..., 1:1 + N_META].astype(v.dtype)
    p_band = p[..., 1 + N_META:].astype(v.dtype)
    o = (jnp.einsum('bnhgqm,bmhd->bnqhgd', p_meta, v_meta)
         + jnp.einsum('bnhgqk,bnkhd->bnqhgd', p_band, vb))
    return o.reshape(B, T, ATTN_WIDTH)[:, META_PAD:]


def fourier_mix(u):
    B, L = u.shape[0], u.shape[1]
    ug = u.astype(jnp.float32).reshape(B, L, N_FOURIER_GROUPS, FOURIER_GROUP_WIDTH)
    f = jnp.fft.fft2(ug, axes=(1, 3), norm="ortho").real
    return f.reshape(B, L, FOURIER_WIDTH).astype(u.dtype)


def mixer_block(h, pos, norm_g, w_in, b_gate, sink, w_attn_out, w_fourier, w_out):
    B, L = h.shape[0], h.shape[1]
    z = rms_norm(h, norm_g) @ w_in
    q, k, v, u, g = jnp.split(z, SPLITS, axis=-1)
    q = rope_partial(q.reshape(B, L, N_Q_HEADS, HEAD_DIM), pos)
    k = rope_partial(k.reshape(B, L, N_KV_HEADS, HEAD_DIM), pos)
    v = v.reshape(B, L, N_KV_HEADS, HEAD_DIM)
    a = window_attention(q, k, v, sink) @ w_attn_out
    f = fourier_mix(u) @ w_fourier
    gates = jax.nn.sigmoid((g + b_gate).astype(jnp.float32)).astype(h.dtype)
    merged = gates[..., :D_MODEL] * a + gates[..., D_MODEL:] * f
    return merged @ w_out


def sq_relu_mlp(h, norm_g, w_up, w_down):
    t = rms_norm(h, norm_g) @ w_up
    return jnp.square(jax.nn.relu(t)) @ w_down


def trunk(x, meta_tokens, norm_mix_g, w_in, b_gate, attn_sink, w_attn_out, w_fourier, w_out,
          norm_mlp_g, w_mlp_up, w_mlp_down, norm_final_g):
    B = x.shape[0]
    meta = jnp.broadcast_to(meta_tokens[None].astype(x.dtype), (B, N_META, D_MODEL))
    h = jnp.concatenate([meta, x], axis=1)
    pos = jnp.arange(h.shape[1], dtype=jnp.float32)
    for l in range(DEPTH):
        h = h + mixer_block(h, pos, norm_mix_g[l], w_in[l], b_gate[l], attn_sink[l],
                            w_attn_out[l], w_fourier[l], w_out[l])
        h = h + sq_relu_mlp(h, norm_mlp_g[l], w_mlp_up[l], w_mlp_down[l])
    h = rms_norm(h, norm_final_g)
    return h[:, N_META:]


def setup_inputs(seed: int = 0) -> dict:
    key = jax.random.key(seed)
    ks = jax.random.split(key, 16)
    nrm = lambda k, shape, s: jax.random.normal(k, shape, jnp.float32) * s
    return {
        "x_prompt": nrm(ks[0], (BATCH, SEQ, D_MODEL), 1.0),
        "x_sample": nrm(ks[1], (DEC_BATCH, DEC_SEQ, D_MODEL), 1.0),
        "meta_tokens": nrm(ks[2], (N_META, D_MODEL), 1.0),
        "norm_mix_g": 1.0 + nrm(ks[3], (DEPTH, D_MODEL), 0.02),
        "w_in": nrm(ks[4], (DEPTH, D_MODEL, IN_WIDTH), D_MODEL ** -0.5),
        "b_gate": nrm(ks[5], (DEPTH, N_BRANCHES * D_MODEL), 0.1),
        "attn_sink": nrm(ks[6], (DEPTH, N_Q_HEADS), 0.5),
        "w_attn_out": nrm(ks[7], (DEPTH, ATTN_WIDTH, D_MODEL), ATTN_WIDTH ** -0.5),
        "w_fourier": nrm(ks[8], (DEPTH, FOURIER_WIDTH, D_MODEL), FOURIER_WIDTH ** -0.5),
        "w_out": nrm(ks[9], (DEPTH, D_MODEL, D_MODEL), D_MODEL ** -0.5),
        "norm_mlp_g": 1.0 + nrm(ks[10], (DEPTH, D_MODEL), 0.02),
        "w_mlp_up": nrm(ks[11], (DEPTH, D_MODEL, D_FF), D_MODEL ** -0.5),
        "w_mlp_down": nrm(ks[12], (DEPTH, D_FF, D_MODEL), D_FF ** -0.5),
        "norm_final_g": 1.0 + nrm(ks[13], (D_MODEL,), 0.02),
    }


def reference(x_prompt, x_sample, meta_tokens, norm_mix_g, w_in, b_gate, attn_sink, w_attn_out,
              w_fourier, w_out, norm_mlp_g, w_mlp_up, w_mlp_down, norm_final_g):
    y_prompt = trunk(x_prompt, meta_tokens, norm_mix_g, w_in, b_gate, attn_sink, w_attn_out,
                     w_fourier, w_out, norm_mlp_g, w_mlp_up, w_mlp_down, norm_final_g)
    y_sample = trunk(x_sample, meta_tokens, norm_mix_g, w_in, b_gate, attn_sink, w_attn_out,
                     w_fourier, w_out, norm_mlp_g, w_mlp_up, w_mlp_down, norm_final_g)
    return (y_prompt, y_sample)
```

```python
import numpy as np
import concourse.bass as bass
import concourse.mybir as mybir
from concourse.bass_utils import run_bass_kernel_spmd

F32 = mybir.dt.float32
BF16 = mybir.dt.bfloat16
AF = mybir.ActivationFunctionType
ALU = mybir.AluOpType

D = 1024
SEQ = 2048
DSEQ = 8192
NMETA = 16
LP = SEQ + NMETA
LS = DSEQ + NMETA
DFF = 4096
EPS = 1e-6
THETA = 500000.0
NA_P, NB_P, NBO_P = 24, 86, 86
NA_S, NB_S, NBO_S = 76, 108, 27
OWN = 2048
HALO = 128
SB_BASE = 17408
SB_END = 229376 - 64
NDMASEM = 24


class Buf:
    __slots__ = ("w", "r", "excl")

    def __init__(self, excl=False):
        self.w = None
        self.r = []
        self.excl = excl


class _Rec:
    def __init__(self):
        self.call = None

    def __getattr__(self, name):
        def f(*a, **kw):
            self.call = (name, a, kw)
        return f


def _freeze(fn):
    if fn is None:
        return None
    rec = _Rec()
    fn(rec)
    name, a, kw = rec.call
    return lambda e: getattr(e, name)(*a, **kw)


class Prog:
    ENGS = ("pe", "act", "dve", "pool", "sp")

    def __init__(self):
        self.ops = {e: [] for e in self.ENGS}
        self.cnt = {e: 0 for e in self.ENGS}
        self.seen = {e: {} for e in self.ENGS}
        self.dnames = {"sp": ["d%d" % i for i in range(NDMASEM)], "pool": ["g%d" % i for i in range(NDMASEM)],
                       "bg": ["m%d" % i for i in range(8)]}
        self.dcnt = {n: 0 for e in self.dnames for n in self.dnames[e]}
        self.drr = {"sp": 0, "pool": 0, "bg": 0}

    def _waits(self, eng, deps):
        waits = []
        for (s, c) in sorted(deps):
            if self.seen[eng].get(s, 0) >= c:
                continue
            waits.append((s, c))
            self.seen[eng][s] = c
        return waits

    def _deps(self, eng, r, w):
        deps = set()
        for b in r:
            if b.w is not None:
                deps.add(b.w)
            if b.excl:
                for t in b.r:
                    if t[0] != eng:
                        deps.add(t)
        for b in w:
            if b.w is not None:
                deps.add(b.w)
            for t in b.r:
                if t[0] == eng and eng != "sp":
                    continue
                deps.add(t)
        if eng == "pe":
            deps = {t for t in deps if t[0] != "pe"}
        return deps

    def op(self, eng, fn, r=(), w=(), inc=True):
        deps = self._deps(eng, r, w)
        waits = self._waits(eng, deps)
        if inc:
            self.cnt[eng] += 1
            tok = (eng, self.cnt[eng])
        else:
            tok = (eng, self.cnt[eng] + 1)
        for b in r:
            b.r.append(tok)
        for b in w:
            b.w = tok
            b.r = []
        self.ops[eng].append((waits, _freeze(fn), eng if inc else None, 1))

    def dma(self, eng, fn, r=(), w=(), sempool=None):
        sp_ = sempool or eng
        nm = self.dnames[sp_][self.drr[sp_]]
        self.drr[sp_] = (self.drr[sp_] + 1) % len(self.dnames[sp_])
        deps = self._deps("sp", r, w)
        if self.dcnt[nm] > 0:
            deps.add((nm, self.dcnt[nm]))
        waits = self._waits(eng, deps)
        self.dcnt[nm] += 16
        tok = (nm, self.dcnt[nm])
        for b in r:
            b.r.append(tok)
        for b in w:
            b.w = tok
            b.r = []
        self.ops[eng].append((waits, _freeze(fn), nm, 16))

    def barrier(self, final=False):
        for e in self.ENGS:
            deps = set()
            for x in self.ENGS:
                if x != e and self.cnt[x] > 0:
                    deps.add((x, self.cnt[x]))
            for nm, c in self.dcnt.items():
                if c > 0 and (final or not nm.startswith("m")):
                    deps.add((nm, c))
            waits = self._waits(e, deps)
            if waits:
                self.ops[e].append((waits, None, None, 0))

    def emit(self, eng, e, sems):
        for (waits, fn, incname, incv) in self.ops[eng]:
            for (s, c) in waits:
                e.wait_ge(sems[s], c)
            if fn is None:
                continue
            ins = fn(e)
            if incname is not None:
                ins.then_inc(sems[incname], incv)


class Ring:
    def __init__(self, tiles, bufs=None):
        self.tiles = tiles
        self.bufs = bufs if bufs is not None else [Buf() for _ in tiles]
        self.i = 0

    def next(self):
        t, b = self.tiles[self.i], self.bufs[self.i]
        self.i = (self.i + 1) % len(self.tiles)
        return t, b


class Arena:
    def __init__(self, nc, base, end):
        self.nc, self.base, self.end, self.cur, self.n = nc, base, end, base, 0

    def alloc(self, shape, dtype):
        nbytes = int(np.prod(shape[1:])) * (4 if dtype == F32 else 2)
        nbytes = (nbytes + 63) // 64 * 64
        off = self.cur
        self.cur += nbytes
        assert self.cur <= self.end, ("SBUF arena overflow", self.cur, self.end)
        self.n += 1
        return self.nc.alloc_sbuf_tensor_at("sb%d_%d" % (self.base, self.n), list(shape), dtype, offset=off)

    def sub(self):
        return Arena(self.nc, self.cur, self.end)


def _fourier_tables(L, Na, Nb, Nbo, kstart):
    na = np.arange(Na, dtype=np.int64)[:, None]
    ja = np.arange(Na, dtype=np.int64)[None, :]
    m = (na * (kstart + ja)) % Na
    ang = 2 * np.pi * m / Na
    fa = np.concatenate([np.cos(ang), -np.sin(ang)], axis=1) / np.sqrt(Na)
    nb = np.arange(Nb, dtype=np.int64)[:, None, None]
    kk = kstart + np.arange(Na, dtype=np.int64)[None, :, None] + Na * np.arange(Nbo, dtype=np.int64)[None, None, :]
    phi = 2 * np.pi * ((nb * kk) % L) / L
    tr = np.cos(phi) / np.sqrt(Nb)
    ti = -np.sin(phi) / np.sqrt(Nb)
    tb = np.stack([np.concatenate([tr, ti], axis=2), np.concatenate([-ti, tr], axis=2)], axis=2)
    return fa.astype(np.float32), np.ascontiguousarray(tb.reshape(Nb, -1)).astype(np.float32)


def _rope_tables(pos):
    half = 8
    inv = (np.float32(THETA) ** (-np.arange(half, dtype=np.float32) / np.float32(half))).astype(np.float32)
    ang = (pos.astype(np.float32)[None, :] * inv[:, None]).astype(np.float32)
    c = np.cos(ang).astype(np.float32)
    s = np.sin(ang).astype(np.float32)
    ct = np.concatenate([c, c], axis=0)
    st = np.concatenate([-s, s], axis=0)
    return np.ascontiguousarray(ct), np.ascontiguousarray(st)


def _consts(core):
    r = core % 4
    ident = np.eye(128, dtype=np.float32)
    perm = np.zeros((128, 128), np.float32)
    for m in range(128):
        d = m % 64
        if d < 8:
            perm[m + 8, m] = 1.0
        elif d < 16:
            perm[m - 8, m] = 1.0
    onesh = np.zeros((128, 256), np.float32)
    onesh[:, 0:64] = 1.0
    onesh[:, 192:256] = 1.0
    pj = np.arange(128)[:, None]
    pi = np.arange(128)[None, :]
    mp = (pi <= pj).astype(np.float32)
    mn = (pj <= pi).astype(np.float32)
    mm = np.repeat((np.arange(128)[:, None] < NMETA).astype(np.float32), 128, axis=1)
    masks = np.stack([mp, mn, mp * (1.0 if r > 0 else 0.0), mn * (1.0 if r < 3 else 0.0), mm], axis=1)
    ch = np.arange(128)
    ang = 2 * np.pi * ((ch[:, None] * ch[None, :]) % 128) / 128
    cs = np.concatenate([np.cos(ang), np.sin(ang)], axis=1) / np.sqrt(128.0)
    fa_p, tb_p = _fourier_tables(LP, NA_P, NB_P, NBO_P, NMETA)
    fa_s, tb_s = _fourier_tables(LS, NA_S, NB_S, NBO_S, NMETA + OWN * r)
    cp, sp_ = _rope_tables(np.arange(LP))
    cso, sso = _rope_tables(np.arange(NMETA + OWN * r - HALO, NMETA + OWN * r - HALO + OWN + 2 * HALO))
    return {
        "c_ident": ident, "c_perm": perm, "c_onesh": onesh,
        "c_masks": np.ascontiguousarray(masks.reshape(128, 640)).astype(np.float32),
        "c_cs": cs.astype(np.float32), "c_fa_p": fa_p, "c_tb_p": tb_p, "c_fa_s": fa_s, "c_tb_s": tb_s,
        "c_rc_p": cp, "c_rs_p": sp_, "c_rc_s": cso, "c_rs_s": sso,
    }


class _Stop(Exception):
    pass


def build_program(stop=None):
    nc = bass.Bass("TRN2", target_bir_lowering=False)
    P = Prog()

    def din(name, shape):
        return nc.dram_tensor(name, list(shape), F32, kind="ExternalInput").ap()

    xp = din("xp", [2, SEQ, D])
    xs = din("xs", [DSEQ, D])
    xso = din("xso", [OWN + 2 * HALO, D])
    meta = din("meta", [NMETA, D])
    w_in = din("w_in", [D, 3328])
    b_gate = din("b_gate", [128, 16])
    sinkd = din("sink", [1, 8])
    w_ao = din("w_ao", [512, D])
    w_f = din("w_f", [512, D])
    w_out = din("w_out", [D, D])
    w_up = din("w_up", [D, DFF])
    w_dn = din("w_dn", [DFF, D])
    g1 = din("g1", [1, D])
    g2 = din("g2", [1, D])
    g3 = din("g3", [1, D])
    c_ident = din("c_ident", [128, 128])
    c_perm = din("c_perm", [128, 128])
    c_onesh = din("c_onesh", [128, 256])
    c_masks = din("c_masks", [128, 640])
    c_cs = din("c_cs", [128, 256])
    c_fa_p = din("c_fa_p", [NA_P, 2 * NA_P])
    c_tb_p = din("c_tb_p", [NB_P, NA_P * 4 * NBO_P])
    c_fa_s = din("c_fa_s", [NA_S, 2 * NA_S])
    c_tb_s = din("c_tb_s", [NB_S, NA_S * 4 * NBO_S])
    c_rc_p = din("c_rc_p", [16, LP])
    c_rs_p = din("c_rs_p", [16, LP])
    c_rc_s = din("c_rc_s", [16, OWN + 2 * HALO])
    c_rs_s = din("c_rs_s", [16, OWN + 2 * HALO])
    yp = nc.dram_tensor("yp", [2, SEQ, D], F32, kind="ExternalOutput").ap()
    ys = nc.dram_tensor("ys", [OWN, D], F32, kind="ExternalOutput").ap()
    wup_bf = nc.dram_tensor("wup_bf", [D, DFF], BF16).ap()
    wdn_bf = nc.dram_tensor("wdn_bf", [DFF, D], BF16).ap()
    u_scr = nc.dram_tensor("u_scr", [LS, 512], BF16).ap()
    wkvu_bf = nc.dram_tensor("wkvu_bf", [D, 768], BF16).ap()
    tbp_bf = nc.dram_tensor("tbp_bf", [NB_P, NA_P * 4 * NBO_P], BF16).ap()
    tbs_bf = nc.dram_tensor("tbs_bf", [NB_S, NA_S * 4 * NBO_S], BF16).ap()

    A = Arena(nc, SB_BASE, SB_END)
    Wq = A.alloc([128, 8, 4, 128], BF16); bWq = []
    Wg = A.alloc([128, 8, 2048], BF16); bWg = []
    Wao = A.alloc([128, 4, D], BF16); bWao = []
    Wf = A.alloc([128, 4, D], BF16); bWf = []
    Wout = A.alloc([128, 8, D], BF16); bWout = []
    g1b = A.alloc([128, D], F32); g2b = A.alloc([128, D], F32); g3b = A.alloc([128, D], F32)
    bG = [Buf(), Buf(), Buf()]
    ident = A.alloc([128, 128], BF16); perm = A.alloc([128, 128], BF16)
    onesh = A.alloc([128, 256], BF16); masks = A.alloc([128, 5, 128], BF16)
    sinkb = A.alloc([128, 4, 128], F32); sinke = A.alloc([128, 4], F32); zeros = A.alloc([128, 128], F32)
    bg = A.alloc([128, 16], F32)
    cs = A.alloc([128, 256], BF16)
    fa_p = A.alloc([128, 2 * NA_P], BF16); fa_s = A.alloc([128, 2 * NA_S], BF16)
    bC = []
    ctab = A.alloc([128, 512], F32); stab = A.alloc([128, 512], F32); bTab = [Buf() for _ in range(4)]
    stat_t = A.alloc([128, 32], F32)
    dummy_t = A.alloc([128, 2], F32)
    fT = A.alloc([128, 4, NBO_P * NA_P], BF16); bfT = [Buf() for _ in range(4)]
    kT = A.alloc([128, 19 * 128], BF16); bkT = Buf()
    Vp = A.alloc([128, 19, 192], BF16); bVp = Buf()
    R = A.sub()

    psr = Ring([nc.alloc_psum_tensor("ps%d" % i, [128, 512], F32) for i in range(7)], [Buf(excl=True) for _ in range(7)])
    psA = Ring(psr.tiles[0:4], psr.bufs[0:4])
    psB = Ring(psr.tiles[4:7], psr.bufs[4:7])
    pst_t = nc.alloc_psum_tensor("pst", [128, 8, 128], BF16)
    bpst = Buf(excl=True)
    stat_i = [0]
    stat_b = [Buf() for _ in range(32)]

    def stat():
        i = stat_i[0]
        stat_i[0] = (i + 1) % 32
        return stat_t[:, i:i + 1], stat_b[i]

    def bcast_g(t):
        return t

    def cdma(out_ap, in_ap, r=(), w=(), sempool=None):
        P.dma("pool", lambda e, o=out_ap, i=in_ap: e.dma_start(out=o, in_=i), r, w, sempool=sempool)

    def sdma(out_ap, in_ap, r=(), w=()):
        P.dma("sp", lambda e, o=out_ap, i=in_ap: e.dma_start(out=o, in_=i), r, w)

    def nb(lst):
        b_ = Buf()
        lst.append(b_)
        return [b_]

    cdma(ident[:], c_ident[:, :], w=nb(bC))
    cdma(perm[:], c_perm[:, :], w=nb(bC))
    cdma(onesh[:], c_onesh[:, :], w=nb(bC))
    cdma(masks[:].rearrange("p a b -> p (a b)"), c_masks[:, :], w=nb(bC))
    cdma(cs[:], c_cs[:, :], w=nb(bC))
    cdma(fa_p[0:NA_P, :], c_fa_p[:, :], w=nb(bC))
    cdma(fa_s[0:NA_S, :], c_fa_s[:, :], w=nb(bC))
    sdma(bg[:], b_gate[:, :], w=nb(bC))
    sdma(g1b[:], g1[0:1, :].partition_broadcast(128), w=[bG[0]])
    sdma(g2b[:], g2[0:1, :].partition_broadcast(128), w=[bG[1]])
    sdma(g3b[:], g3[0:1, :].partition_broadcast(128), w=[bG[2]])
    bScr = []
    cdma(wkvu_bf[:, :], w_in[:, 512:1280], w=nb(bScr))
    cdma(tbp_bf[:, :].rearrange("p (x y) -> p x y", x=8), c_tb_p[:, :].rearrange("p (x y) -> p x y", x=8), w=nb(bScr))
    cdma(tbs_bf[:, :].rearrange("p (x y) -> p x y", x=8), c_tb_s[:, :].rearrange("p (x y) -> p x y", x=8), w=nb(bScr))
    bsk = [Buf(), Buf()]
    sdma(sinke[0:64, :], sinkd[0:1, 0:4].partition_broadcast(64), w=[bsk[0]])
    sdma(sinke[64:128, :], sinkd[0:1, 4:8].partition_broadcast(64), w=[bsk[1]])
    bz = nb(bC)
    P.op("pool", lambda e: e.memset(zeros[:], 0.0), w=bz)
    P.op("pool", lambda e: e.memset(ctab[:], 1.0), w=bTab)
    P.op("pool", lambda e: e.memset(stab[:], 0.0), w=bTab)
    P.op("pool", lambda e: e.memset(Vp[:].rearrange("p a b -> p (a b)"), 0.0), w=[bVp])
    P.op("pool", lambda e: e.memset(kT[:, 18 * 128:19 * 128], 0.0), w=[bkT])
    for gi, gt in enumerate((g1b, g2b, g3b)):
        P.op("dve", lambda e, gt=gt: e.tensor_scalar(out=gt[:], in0=gt[:], scalar1=float(np.sqrt(D)), scalar2=None,
                                                      op0=ALU.mult), r=[bG[gi]], w=[bG[gi]])
    P.op("act", lambda e: e.activation(out=sinke[:], in_=sinke[:], func=AF.Exp), r=bsk, w=bsk)
    bsb = nb(bC)
    for g in range(4):
        P.op("dve", lambda e, g=g: e.tensor_scalar(out=sinkb[:, g, :], in0=zeros[:], scalar1=sinke[:, g:g + 1],
                                                    scalar2=None, op0=ALU.add), r=bsk + bz, w=bsb)
    bWupD, bWdnD = [], []

    def setup_late():
        w_in_v = w_in.rearrange("(kt p) c -> p kt c", p=128)
        for g in range(4):
            for hh in range(2):
                c0 = (hh * 4 + g) * 64
                cdma(Wq[:, :, g, hh * 64:(hh + 1) * 64], w_in_v[:, :, c0:c0 + 64], w=nb(bWq))
        for kt in range(8):
            cdma(Wg[:, kt, :], w_in[kt * 128:(kt + 1) * 128, 1280:3328], w=nb(bWg))
        for g in range(4):
            for hh in range(2):
                r0 = (hh * 4 + g) * 64
                cdma(Wao[hh * 64:(hh + 1) * 64, g, :], w_ao[r0:r0 + 64, :], w=nb(bWao))
        for gr in range(4):
            cdma(Wf[:, gr, :], w_f[gr * 128:(gr + 1) * 128, :], w=nb(bWf))
        for kt in range(8):
            cdma(Wout[:, kt, :], w_out[kt * 128:(kt + 1) * 128, :], w=nb(bWout))
        for i in range(8):
            cdma(wup_bf[i * 128:(i + 1) * 128, :].rearrange("p (a b) -> p a b", b=1024),
                 w_up[i * 128:(i + 1) * 128, :].rearrange("p (a b) -> p a b", b=1024), w=nb(bWupD), sempool="bg")
        for i in range(32):
            cdma(wdn_bf[i * 128:(i + 1) * 128, :], w_dn[i * 128:(i + 1) * 128, :], w=nb(bWdnD), sempool="bg")


    def rms_scale(x_ap, bx, n, gb_t, out_ap, bout, junk_ap, bjunk):
        ssq, bs = stat()
        sq, bq = stat()
        rs, br = stat()
        bxl = bx if isinstance(bx, list) else [bx]
        P.op("act", lambda e: e.activation(out=junk_ap, in_=x_ap, func=AF.Square, accum_out=ssq[0:n, :]),
             r=bxl, w=[bjunk, bs])
        P.op("act", lambda e: e.activation(out=sq[0:n, :], in_=ssq[0:n, :], func=AF.Sqrt, bias=float(D * EPS)),
             r=[bs], w=[bq])
        P.op("dve", lambda e: e.reciprocal(out=rs[0:n, :], in_=sq[0:n, :]), r=[bq], w=[br])
        P.op("dve", lambda e: e.scalar_tensor_tensor(out=out_ap, in0=x_ap, scalar=rs[0:n, :], in1=gb_t[0:n, :],
                                                     op0=ALU.mult, op1=ALU.mult), r=bxl + [br] + bG, w=[bout])

    def transposes(src_t, bsrc, n, dst_ap, bdst, evac_eng):
        for kt in range(8):
            P.op("pe", lambda e, kt=kt: e.transpose(out=pst_t[:, kt, 0:n], in_=src_t[0:n, kt * 128:(kt + 1) * 128],
                                                     identity=ident[0:n, 0:n]),
                 r=[bsrc] + bC, w=[bpst], inc=(kt == 7))
        if evac_eng == "act":
            P.op("act", lambda e: e.activation(out=dst_ap, in_=pst_t[:, :, 0:n], func=AF.Copy), r=[bpst], w=[bdst])
        else:
            P.op("dve", lambda e: e.tensor_copy(out=dst_ap, in_=pst_t[:, :, 0:n]), r=[bpst], w=[bdst])

    def load_rope(rc, rs, p0, n):
        i = 0
        for (tab, src) in ((ctab, rc), (stab, rs)):
            for base in (0, 64):
                sdma(tab[base:base + 16, 0:n], src[:, p0:p0 + n], w=[bTab[i]])
                i += 1

    def rope(ps_raw, braw, n, tabs, out_ap, bout, tmpr):
        rawb, brb = tmpr["rawb"].next()
        t1, b1 = tmpr["f32"].next()
        t2, b2 = tmpr["f32"].next()
        ps2, bp2 = psr.next()
        P.op("act", lambda e: e.activation(out=rawb[:, 0:n], in_=ps_raw, func=AF.Copy), r=[braw], w=[brb])
        P.op("pe", lambda e: e.matmul(ps2[:, 0:n], lhsT=perm[:], rhs=rawb[:, 0:n], start=True, stop=True),
             r=[brb] + bC, w=[bp2])
        if stop == 'p2b2' and n == 512:
            raise _Stop()
        ct_ap, st_ap, btabs = tabs
        P.op("dve", lambda e: e.tensor_tensor(out=t1[:, 0:n], in0=ps_raw, in1=ct_ap, op=ALU.mult),
             r=[braw] + btabs, w=[b1])
        P.op("dve", lambda e: e.tensor_tensor(out=t2[:, 0:n], in0=ps2[:, 0:n], in1=st_ap, op=ALU.mult),
             r=[bp2] + btabs, w=[b2])
        if stop == 'p2b3' and n == 512:
            raise _Stop()
        if isinstance(out_ap, tuple):
            qp, g_ = out_ap
            P.op("pool", lambda e: e.tensor_tensor(out=qp[0][0:64, g_, :], in0=t1[0:64, 0:n], in1=t2[0:64, 0:n], op=ALU.add),
                 r=[b1, b2], w=[bout[0]])
            P.op("dve", lambda e: e.tensor_tensor(out=qp[1][64:128, g_, :], in0=t1[64:128, 0:n], in1=t2[64:128, 0:n], op=ALU.add),
                 r=[b1, b2], w=[bout[1]])
        else:
            P.op("pool", lambda e: e.tensor_tensor(out=out_ap, in0=t1[:, 0:n], in1=t2[:, 0:n], op=ALU.add),
                 r=[b1, b2], w=[bout])

    def phase1(tiles, tb_spec):
        P.barrier()
        a = R.sub()
        (Na_, Nbo_, Nb_, c_tb_) = tb_spec
        tb_bytes = Na_ * 2 * 2 * Nbo_ * 2
        TB_ = nc.alloc_sbuf_tensor_at("tb%d" % P.cnt["pe"], [128, Na_, 2, 2 * Nbo_], BF16, offset=SB_END - tb_bytes)
        bTB_ = Buf()
        wtab = sum(t[1] for t in tiles if t[3] is not None)
        ctL = a.alloc([128, wtab], F32); stL = a.alloc([128, wtab], F32)
        btL = [Buf(), Buf()]
        P.op("pool", lambda e: e.memset(ctL[:], 1.0), w=[btL[0]])
        P.op("pool", lambda e: e.memset(stL[:], 0.0), w=[btL[1]])
        runs = []
        col = 0
        tabcol = {}
        for ti, t in enumerate(tiles):
            if t[3] is None:
                continue
            rc_, rs_, p0_ = t[4]
            tabcol[ti] = col
            if runs and runs[-1][0] is rc_ and runs[-1][2] + runs[-1][4] == p0_:
                runs[-1][4] += t[1]
            else:
                runs.append([rc_, rs_, p0_, col, t[1]])
            col += t[1]
        Wk = a.alloc([128, 8, 128], BF16); Wv = a.alloc([128, 8, 128], BF16); Wu = a.alloc([128, 8, 512], BF16)
        bWk, bWv, bWu = [], [], []
        xst = Ring([a.alloc([128, D], F32) for _ in range(4)])
        junk = a.alloc([128, D], BF16); bjunk = Buf()
        xnr = Ring([a.alloc([128, D], BF16) for _ in range(3)])
        xnTr = Ring([a.alloc([128, 8, 128], BF16) for _ in range(3)])
        ustr = Ring([a.alloc([128, 512], BF16) for _ in range(3)])
        tmpr = {"rawb": Ring([a.alloc([128, 128], BF16) for _ in range(2)]),
                "f32": Ring([a.alloc([128, 128], F32) for _ in range(4)])}
        btabs = []

        def emit_loads():
            wkvu_v = wkvu_bf.rearrange("(kt p) c -> p kt c", p=128)
            sdma(Wu[:, :, :], wkvu_v[:, :, 256:768], r=bScr, w=nb(bWu))
            sdma(Wv[:, :, :], wkvu_v[:, :, 128:256], r=bScr, w=nb(bWv))
            sdma(Wk[:, :, :], wkvu_v[:, :, 0:128], r=bScr, w=nb(bWk))
            for (rc_, rs_, p0_, c0_, n_) in runs:
                for (tab_, src_, bi_) in ((ctL, rc_, 0), (stL, rs_, 1)):
                    for base in (0, 64):
                        sdma(tab_[base:base + 16, c0_:c0_ + n_], src_[:, p0_:p0_ + n_], r=[btL[bi_]], w=nb(btabs))

        def emit_tb():
            sdma(TB_[0:Nb_].rearrange("p a b c -> p (a b c)"), c_tb_[:, :], r=bScr, w=[bTB_])
        def stage1(tile):
            (src, n, urow, kvb, ropei, ti) = tile
            xt, bx = xst.next()
            sdma(xt[0:n, :], src, w=[bx])
            xn, bxn = xnr.next()
            rms_scale(xt[0:n, :], bx, n, g1b, xn[0:n, :], bxn, junk[0:n, :], bjunk)
            return (tile, xn, bxn)

        deferred = []

        def run_deferred():
            while deferred:
                deferred.pop(0)()

        def stage2(tile, xn, bxn):
            (src, n, urow, kvb, ropei, ti) = tile
            xnT, bxT = xnTr.next()
            transposes(xn, bxn, n, xnT[:, :, 0:n], bxT, "act")
            if urow is not None:
                ps, bp = psr.next()
                for kt in range(8):
                    P.op("pe", lambda e: e.matmul(ps[0:n, :], lhsT=xnT[:, kt, 0:n], rhs=Wu[:, kt, :],
                                                  start=(kt == 0), stop=(kt == 7)),
                         r=[bxT] + bWu, w=[bp], inc=(kt == 7))
                us, bus = ustr.next()
                P.op("dve", lambda e: e.tensor_copy(out=us[0:n, :], in_=ps[0:n, :]), r=[bp], w=[bus])
                cdma(u_scr[urow:urow + n, :], us[0:n, :], r=[bus], w=[Buf()])
            if kvb is not None:
                ps, bp = psr.next()
                for kt in range(8):
                    P.op("pe", lambda e: e.matmul(ps[0:n, 0:128], lhsT=xnT[:, kt, 0:n], rhs=Wv[:, kt, :],
                                                  start=(kt == 0), stop=(kt == 7)),
                         r=[bxT] + bWv, w=[bp], inc=(kt == 7))
                P.op("act", lambda e: e.activation(out=Vp[0:n, kvb, 0:64], in_=ps[0:n, 0:64], func=AF.Copy),
                     r=[bp], w=[bVp])
                P.op("act", lambda e: e.activation(out=Vp[0:n, kvb, 128:192], in_=ps[0:n, 64:128], func=AF.Copy),
                     r=[bp], w=[bVp])
                psk, bpk = psr.next()
                for kt in range(8):
                    P.op("pe", lambda e: e.matmul(psk[:, 0:n], lhsT=Wk[:, kt, :], rhs=xnT[:, kt, 0:n],
                                                  start=(kt == 0), stop=(kt == 7)),
                         r=[bxT] + bWk, w=[bpk], inc=(kt == 7))
                c0 = tabcol[ti]
                run_deferred()
                deferred.append(lambda: rope(psk[:, 0:n], bpk, n, (ctL[:, c0:c0 + n], stL[:, c0:c0 + n], btabs),
                                             kT[:, kvb * 128:kvb * 128 + n], bkT, tmpr))
            else:
                run_deferred()

        pend = []
        for ti, tile in enumerate(tiles):
            pend.append(stage1(tuple(tile) + (ti,)))
            if ti == min(2, len(tiles) - 1):
                emit_loads()
            if ti == min(6, len(tiles) - 1):
                emit_tb()
            if len(pend) > 2:
                stage2(*pend.pop(0))
        while pend:
            stage2(*pend.pop(0))
        run_deferred()
        return TB_, bTB_

    late_pending = [True]

    def fourier(L, Na, Nb, Nbo, fa_t, TB, bTB, CW):
        P.barrier()
        if late_pending[0]:
            late_pending[0] = False
            setup_late()
        a = R.sub()
        nblk = 128 // CW
        ugr = Ring([a.alloc([128, Nb, CW], BF16) for _ in range(2)])
        Yh = a.alloc([128, CW, 2 * Na], BF16); bYh = Buf()
        Vg = a.alloc([128, 2, Na, Nbo], BF16); bVg = Buf()
        ev = [0]

        def evac(out_ap, in_ap, r, w):
            ev[0] += 1
            if ev[0] % 2:
                P.op("act", lambda e: e.activation(out=out_ap, in_=in_ap, func=AF.Copy), r=r, w=w)
            else:
                P.op("dve", lambda e: e.tensor_copy(out=out_ap, in_=in_ap), r=r, w=w)

        u_v = u_scr[0:L, :].rearrange("(na nb) c -> na nb c", nb=Nb)
        nA = 512 // (2 * Na)
        nB = 512 // (2 * Nbo)
        slots = {}

        def load_u(i):
            if i >= 4 * nblk or i in slots:
                return
            g_, hf_ = i // nblk, i % nblk
            t_, b_ = ugr.next()
            slots[i] = (t_, b_)
            sdma(t_[0:Na, :, :], u_v[:, :, g_ * 128 + hf_ * CW:g_ * 128 + hf_ * CW + CW], w=[b_])

        def stage_a(g, hf):
            load_u(g * nblk + hf + 1)
            Ug, bUg = slots[g * nblk + hf]
            c = 0
            while c < CW:
                nch = min(nA, CW - c)
                ps, bp = psr.next()
                for j in range(nch):
                    P.op("pe", lambda e: e.matmul(ps[0:Nb, j * 2 * Na:(j + 1) * 2 * Na],
                                                  lhsT=Ug[0:Na, :, c + j], rhs=fa_t[0:Na, :], start=True, stop=True),
                         r=[bUg] + bC, w=[bp], inc=(j == nch - 1))
                evac(Yh[0:Nb, c:c + nch, :], ps[0:Nb, 0:nch * 2 * Na].rearrange("p (a b) -> p a b", b=2 * Na),
                     [bp], [bYh])
                c += nch

        def stage_b(g, hf):
            ja = 0
            while ja < Na:
                nj = min(nB, Na - ja)
                ps, bp = psr.next()
                for j in range(nj):
                    for cc in range(2):
                        P.op("pe", lambda e: e.matmul(
                            ps[hf * CW:(hf + 1) * CW, j * 2 * Nbo:(j + 1) * 2 * Nbo], lhsT=Yh[0:Nb, :, cc * Na + ja + j],
                            rhs=TB[0:Nb, ja + j, cc, :], start=(cc == 0), stop=(cc == 1)),
                             r=[bYh, bTB], w=[bp], inc=(j == nj - 1 and cc == 1))
                evac(Vg[hf * CW:(hf + 1) * CW, :, ja:ja + nj, :].rearrange("p c a b -> p a c b"),
                     ps[hf * CW:(hf + 1) * CW, 0:nj * 2 * Nbo].rearrange("p (a c b) -> p a c b", c=2, b=Nbo), [bp], [bVg])
                ja += nj

        def stage_c(g):
            ca = 512 // Nbo
            fTg = fT[:, g, 0:Nbo * Na].rearrange("p (b a) -> p b a", a=Na)
            ja = 0
            while ja < Na:
                k = min(ca, Na - ja)
                ps, bp = psr.next()
                P.op("pe", lambda e: e.matmul(ps[:, 0:k * Nbo], lhsT=cs[:, 0:128],
                                              rhs=Vg[:, 0, ja:ja + k, :].rearrange("p a b -> p (a b)"),
                                              start=True, stop=False), r=[bVg] + bC, w=[bp], inc=False)
                P.op("pe", lambda e: e.matmul(ps[:, 0:k * Nbo], lhsT=cs[:, 128:256],
                                              rhs=Vg[:, 1, ja:ja + k, :].rearrange("p a b -> p (a b)"),
                                              start=False, stop=True), r=[bVg] + bC, w=[bp])
                evac(fTg[:, :, ja:ja + k].rearrange("p b a -> p a b"),
                     ps[:, 0:k * Nbo].rearrange("p (a b) -> p a b", b=Nbo), [bp], [bfT[g]])
                ja += k

        load_u(0)
        for g in range(4):
            for hf in range(nblk):
                stage_a(g, hf)
                if hf == 0 and g > 0:
                    stage_c(g - 1)
                stage_b(g, hf)
        stage_c(3)

    def phase2(x_src, y_dst, rc, rs, pos_base, kvoff, nkb, edge_masks):
        P.barrier()
        a = R.sub()
        xh = a.alloc([128, 4, D], F32); bxh = [Buf() for _ in range(4)]
        xnT = a.alloc([128, 8, 512], BF16); bxnT = [Buf() for _ in range(4)]
        wupr = Ring([a.alloc([128, 8, 256], BF16) for _ in range(2)])
        wdnr = Ring([a.alloc([128, 4, D], BF16) for _ in range(2)])
        relu_off = a.cur
        relur = Ring([a.alloc([128, 512], F32) for _ in range(2)])
        actT = a.alloc([128, 32, 512], BF16); bact = [Buf() for _ in range(32)]
        b = Arena(nc, a.cur - 32 * 512 * 2, a.cur)
        xnr = Ring([b.alloc([128, D], BF16) for _ in range(2)])
        qTp = [b.alloc([128, 4, 512], BF16) for _ in range(2)]; bq = [Buf() for _ in range(8)]
        aT = b.alloc([128, 4, 512], BF16); baT = [Buf() for _ in range(4)]
        mT = b.alloc([128, 8, 512], BF16); bm = [Buf() for _ in range(8)]
        ptr = Ring([b.alloc([128, 512], BF16) for _ in range(3)])
        tmpr = {"rawb": ptr,
                "f32": None}
        f32_own = Ring([b.alloc([128, 512], F32) for _ in range(2)])
        tmpr["f32"] = Ring(f32_own.tiles + relur.tiles, f32_own.bufs + relur.bufs)
        xstage = nc.alloc_sbuf_tensor_at("xstage%d" % a.base, [128, D], F32, offset=relu_off)
        bxs = relur.bufs
        alias_all = bact + bq + baT + bm + ptr.bufs + tmpr["f32"].bufs + xnr.bufs

        class Stream:
            def __init__(self, ring, total, loader):
                self.ring, self.total, self.loader, self.loaded, self.slots = ring, total, loader, 0, {}

            def ensure(self, j):
                while self.loaded < min(j + 2, self.total):
                    t, bb = self.ring.next()
                    self.slots[self.loaded] = (t, bb)
                    self.loader(self.loaded, t, bb)
                    self.loaded += 1

            def get(self, j):
                self.ensure(j)
                return self.slots[j]

        def load_up(j, t, bb):
            i = j % 16
            sdma(t[:], wup_bf[:, i * 256:(i + 1) * 256].rearrange("(kt p) c -> p kt c", p=128), r=bWupD, w=[bb])

        def load_dn(j, t, bb):
            i = j % 8
            sdma(t[:], wdn_bf[i * 512:(i + 1) * 512, :].rearrange("(f p) c -> p f c", p=128), r=bWdnD, w=[bb])

        sup = Stream(wupr, 4 * 16, load_up)
        sdn = Stream(wdnr, 4 * 8, load_dn)

        def final_norm(st):
            for t in range(4):
                ssq, bs = stat()
                sq, bq2 = stat()
                rs_, br = stat()
                jk, bjk = xnr.next()
                P.op("act", lambda e: e.activation(out=jk[:, :], in_=xh[:, t, :], func=AF.Square, accum_out=ssq),
                     r=[bxh[t]], w=[bjk, bs])
                P.op("act", lambda e: e.activation(out=sq, in_=ssq, func=AF.Sqrt, bias=float(D * EPS)),
                     r=[bs], w=[bq2])
                P.op("dve", lambda e: e.reciprocal(out=rs_, in_=sq), r=[bq2], w=[br])
                P.op("dve", lambda e: e.scalar_tensor_tensor(out=xh[:, t, :], in0=xh[:, t, :], scalar=rs_,
                                                             in1=g3b[:], op0=ALU.mult, op1=ALU.mult),
                     r=[bxh[t], br] + bG, w=[bxh[t]])
                sdma(y_dst(st * 4 + t), xh[:, t, :], r=[bxh[t]])
            if st < 3:
                for t in range(4):
                    sdma(xh[:, t, :], x_src((st + 1) * 4 + t), w=[bxh[t]])

        for st in range(4):
            if st == 0:
                for t in range(4):
                    sdma(xh[:, t, :], x_src(st * 4 + t), w=[bxh[t]])
                pend = []
                for t in range(5):
                    if t < 4:
                        xn, bxn = xnr.next()
                        rms_scale(xh[:, t, :], bxh[t], 128, g1b, xn[:, :], bxn, xn[:, :], bxn)
                        pend.append((t, xn, bxn))
                    if t >= 1:
                        (t2, xn2, bxn2) = pend.pop(0)
                        transposes(xn2, bxn2, 128, xnT[:, :, t2 * 128:(t2 + 1) * 128], bxnT[t2], "dve" if t2 % 2 else "act")
                load_rope(rc, rs, pos_base + st * 512, 512)
            if stop == 'p2a':
                raise _Stop()
            P.op("pool", lambda e: e.memset(qTp[0][64:128].rearrange("p a b -> p (a b)"), 0.0), w=bq[0:4])
            P.op("pool", lambda e: e.memset(qTp[1][0:64].rearrange("p a b -> p (a b)"), 0.0), w=bq[4:8])
            qps = []
            for g in range(4):
                ps, bp = psr.next()
                for kt in range(8):
                    P.op("pe", lambda e: e.matmul(ps[:, :], lhsT=Wq[:, kt, g, :], rhs=xnT[:, kt, :],
                                                  start=(kt == 0), stop=(kt == 7)),
                         r=bxnT + bWq, w=[bp], inc=(kt == 7))
                qps.append((ps, bp))
            for g in range(4):
                ps, bp = qps[g]
                rope(ps[:, :], bp, 512, (ctab[:, 0:512], stab[:, 0:512], bTab), (qTp, g), (bq[g], bq[4 + g]), tmpr)
            if st >= 1:
                final_norm(st - 1)
            if stop == 'p2b':
                raise _Stop()
            items = []
            for qb in range(4):
                gq = st * 4 + qb
                kb_c = gq + kvoff
                blocks = [(18, 128, 4)]
                if kb_c - 1 >= 0:
                    blocks.append((kb_c - 1, 128, (2 if (edge_masks and gq == 0) else 0)))
                blocks.append((kb_c, 128, None))
                if kb_c + 1 < nkb:
                    blocks.append((kb_c + 1, 128, (3 if (edge_masks and gq == 15) else 1)))
                n_it = 2 * len(blocks)
                k_it = 0
                for h in range(2):
                    for (kvb, nk, mi) in blocks:
                        items.append(dict(qb=qb, h=h, kvb=kvb, nk=nk, mi=mi, first=(k_it == 0), last=(k_it == n_it - 1)))
                        k_it += 1

            def score(it):
                ps, bp = psB.next()
                it["ps"], it["bp"] = ps, bp
                h, nk, qb = it["h"], it["nk"], it["qb"]
                kcol = it["kvb"] * 128
                P.op("pe", lambda e: e.matmul(ps[0:nk, :], lhsT=kT[:, kcol:kcol + nk],
                                              rhs=qTp[h][:, :, qb * 128:(qb + 1) * 128], start=True, stop=True),
                     r=[bkT] + bq, w=[bp])

            LOOK = 2
            for i in range(min(LOOK, len(items))):
                score(items[i])
            acc = None
            for i, it in enumerate(items):
                if i + LOOK < len(items):
                    score(items[i + LOOK])
                if it["first"]:
                    pnum, bnum = psA.next()
                    pden, bden = psA.next()
                    acc = (pnum, bnum, pden, bden)
                pnum, bnum, pden, bden = acc
                ps, bp, nk, h, kvb, mi, qb = it["ps"], it["bp"], it["nk"], it["h"], it["kvb"], it["mi"], it["qb"]
                pt, bpt = ptr.next()
                P.op("act", lambda e: e.activation(out=pt[0:nk, :], in_=ps[0:nk, :], func=AF.Exp, scale=0.125), r=[bp], w=[bpt])
                if mi is not None:
                    mk = bass.AP(masks, mi * 128, [[640, 128], [0, 4], [1, 128]])
                    P.op("dve" if mi == 4 else "pool", lambda e: e.tensor_tensor(
                        out=pt[:, :].rearrange("p (g q) -> p g q", g=4), in0=pt[:, :].rearrange("p (g q) -> p g q", g=4),
                        in1=mk, op=ALU.mult), r=[bpt] + bC, w=[bpt])
                first, last = it["first"], it["last"]
                P.op("pe", lambda e: e.matmul(pnum[:, :], lhsT=Vp[0:nk, kvb, h * 64:h * 64 + 128], rhs=pt[0:nk, :],
                                              start=first, stop=last), r=[bpt, bVp], w=[bnum], inc=last)
                P.op("pe", lambda e: e.matmul(pden[:, :], lhsT=onesh[0:nk, h * 128:(h + 1) * 128], rhs=pt[0:nk, :],
                                              start=first, stop=last), r=[bpt] + bC, w=[bden], inc=True)
                if last:
                    t1, b1 = tmpr["f32"].next()
                    P.op("dve", lambda e: e.tensor_tensor(out=t1[:, :], in0=pden[:, :],
                                                          in1=sinkb[:].rearrange("p g q -> p (g q)"), op=ALU.add),
                         r=[bden] + bC, w=[b1])
                    P.op("dve", lambda e: e.reciprocal(out=t1[:, :], in_=t1[:, :]), r=[b1], w=[b1])
                    P.op("dve", lambda e: e.tensor_tensor(
                        out=aT[:, :, qb * 128:(qb + 1) * 128], in0=pnum[:, :].rearrange("p (g q) -> p g q", g=4),
                        in1=t1[:, :].rearrange("p (g q) -> p g q", g=4), op=ALU.mult), r=[bnum, b1], w=[baT[qb]])
            if stop == 'p2c':
                raise _Stop()
            for c in range(8):
                pga, bga = psA.next()
                pgf, bgf = psA.next()
                pa, bpa = psB.next()
                pf, bpf = psB.next()
                for (pg, bgx, off) in ((pga, bga, 0), (pgf, bgf, 1024)):
                    for kt in range(8):
                        P.op("pe", lambda e, kt=kt, pg=pg, off=off, c=c: e.matmul(
                            pg[:, :], lhsT=Wg[:, kt, off + c * 128:off + (c + 1) * 128], rhs=xnT[:, kt, :],
                            start=(kt == 0), stop=(kt == 7)), r=bxnT + bWg, w=[bgx], inc=(kt == 7))
                for g in range(4):
                    P.op("pe", lambda e, g=g, c=c, pa=pa: e.matmul(pa[:, :], lhsT=Wao[:, g, c * 128:(c + 1) * 128], rhs=aT[:, g, :],
                                                                    start=(g == 0), stop=(g == 3)),
                         r=baT + bWao, w=[bpa], inc=(g == 3))
                for g in range(4):
                    P.op("pe", lambda e, g=g, c=c, pf=pf: e.matmul(pf[:, :], lhsT=Wf[:, g, c * 128:(c + 1) * 128],
                                                                    rhs=fT[:, g, st * 512:(st + 1) * 512],
                                                                    start=(g == 0), stop=(g == 3)),
                         r=bfT + bWf, w=[bpf], inc=(g == 3))
                sa, bsa = tmpr["f32"].next()
                sf, bsf = tmpr["f32"].next()
                P.op("act", lambda e, pga=pga, sa=sa, c=c: e.activation(out=sa[:, :], in_=pga[:, :], func=AF.Sigmoid,
                                                                        bias=bg[:, c:c + 1]), r=[bga] + bC, w=[bsa])
                P.op("act", lambda e, pgf=pgf, sf=sf, c=c: e.activation(out=sf[:, :], in_=pgf[:, :], func=AF.Sigmoid,
                                                                        bias=bg[:, 8 + c:9 + c]), r=[bgf] + bC, w=[bsf])
                P.op("dve", lambda e, pa=pa, sa=sa: e.tensor_tensor(out=sa[:, :], in0=pa[:, :], in1=sa[:, :], op=ALU.mult),
                     r=[bpa, bsa], w=[bsa])
                P.op("dve", lambda e, pf=pf, sf=sf: e.tensor_tensor(out=sf[:, :], in0=pf[:, :], in1=sf[:, :], op=ALU.mult),
                     r=[bpf, bsf], w=[bsf])
                P.op("pool", lambda e, sa=sa, sf=sf, c=c: e.tensor_tensor(out=mT[:, c, :], in0=sa[:, :], in1=sf[:, :], op=ALU.add),
                     r=[bsa, bsf], w=[bm[c]])
            if stop == 'p2d':
                raise _Stop()
            for t in range(4):
                for hf in range(2):
                    ps, bp = psr.next()
                    for c in range(8):
                        P.op("pe", lambda e, c=c, t=t, hf=hf, ps=ps: e.matmul(
                            ps[:, :], lhsT=mT[:, c, t * 128:(t + 1) * 128], rhs=Wout[:, c, hf * 512:(hf + 1) * 512],
                            start=(c == 0), stop=(c == 7)), r=bm + bWout, w=[bp], inc=(c == 7))
                    P.op("dve", lambda e, t=t, hf=hf, ps=ps: e.tensor_tensor(
                        out=xh[:, t, hf * 512:(hf + 1) * 512], in0=ps[:, :], in1=xh[:, t, hf * 512:(hf + 1) * 512], op=ALU.add),
                         r=[bp, bxh[t]], w=[bxh[t]])
            if stop == 'p2e':
                raise _Stop()
            sup.ensure(st * 16)
            pend = []
            for t in range(5):
                if t < 4:
                    xn, bxn = xnr.next()
                    rms_scale(xh[:, t, :], bxh[t], 128, g2b, xn[:, :], bxn, xn[:, :], bxn)
                    pend.append((t, xn, bxn))
                if t >= 1:
                    (t2, xn2, bxn2) = pend.pop(0)
                    transposes(xn2, bxn2, 128, xnT[:, :, t2 * 128:(t2 + 1) * 128], bxnT[t2], "dve" if t2 % 2 else "act")
            sdn.ensure(st * 8)
            for i in range(16):
                wt, bw = sup.get(st * 16 + i)
                for f in range(2):
                    ffc = i * 2 + f
                    ps, bp = psr.next()
                    for kt in range(8):
                        P.op("pe", lambda e, kt=kt, f=f, wt=wt, ps=ps: e.matmul(
                            ps[:, :], lhsT=wt[:, kt, f * 128:(f + 1) * 128], rhs=xnT[:, kt, :],
                            start=(kt == 0), stop=(kt == 7)), r=bxnT + [bw], w=[bp], inc=(kt == 7))
                    rl, brl = relur.next()
                    P.op("act", lambda e, ps=ps, rl=rl: e.activation(out=rl[:, :], in_=ps[:, :], func=AF.Relu), r=[bp], w=[brl])
                    P.op("dve", lambda e, rl=rl, ffc=ffc: e.tensor_tensor(out=actT[:, ffc, :], in0=rl[:, :], in1=rl[:, :],
                                                                           op=ALU.mult),
                         r=[brl] + (alias_all if ffc == 0 else []), w=[bact[ffc]] + (alias_all if ffc == 0 else []))
            if stop == 'p2f':
                raise _Stop()
            if st < 3:
                sup.ensure((st + 1) * 16)
            pre = []
            for i in range(8):
                wt, bw = sdn.get(st * 8 + i)
                for t in range(4):
                    for hf in range(2):
                        ps, bp = psr.next()
                        for f in range(4):
                            P.op("pe", lambda e: e.matmul(
                                ps[:, :], lhsT=actT[:, i * 4 + f, t * 128:(t + 1) * 128], rhs=wt[:, f, hf * 512:(hf + 1) * 512],
                                start=(f == 0), stop=(f == 3)), r=[bact[i * 4 + f], bw], w=[bp], inc=(f == 3))
                        P.op("dve", lambda e: e.tensor_tensor(
                            out=xh[:, t, hf * 512:(hf + 1) * 512], in0=ps[:, :], in1=xh[:, t, hf * 512:(hf + 1) * 512],
                            op=ALU.add), r=[bp, bxh[t]], w=[bxh[t]])
                if st < 3:
                    if i == 0:
                        P.op("dve", lambda e: e.memset(dummy_t[:, 1:2], 0.0), r=bact[0:4], w=xnr.bufs)
                        load_rope(rc, rs, pos_base + (st + 1) * 512, 512)
                    if i in (0, 2, 4, 6):
                        tn = i // 2
                        cdma(xstage[:, :], x_src((st + 1) * 4 + tn), w=bxs)
                        xn, bxn = xnr.next()
                        rms_scale(xstage[:, :], bxs, 128, g1b, xn[:, :], bxn, xn[:, :], bxn)
                        pre.append((tn, xn, bxn))
                    if i in (1, 3, 5, 7):
                        (t2, xn2, bxn2) = pre.pop(0)
                        transposes(xn2, bxn2, 128, xnT[:, :, t2 * 128:(t2 + 1) * 128], bxnT[t2], "act")
            P.op("dve", lambda e: e.memset(dummy_t[:, 0:1], 0.0), r=bact, w=alias_all)
            if stop == 'p2g':
                raise _Stop()
            if st == 3:
                final_norm(3)

    def run_all():
        for s_ in range(2):
            tiles = []
            for j in range(16):
                tiles.append((xp[s_, j * 128:(j + 1) * 128, :], 128, NMETA + j * 128, j, (c_rc_p, c_rs_p, NMETA + j * 128)))
            tiles.append((meta[:, :], NMETA, 0, 18, (c_rc_p, c_rs_p, 0)))
            if stop == "setup":
                return
            TBt, bTBt = phase1(tiles, (NA_P, NBO_P, NB_P, tbp_bf))
            if stop == "p1":
                return
            fourier(LP, NA_P, NB_P, NBO_P, fa_p, TBt, bTBt, 128)
            if stop == "f":
                return
            phase2(lambda t, s_=s_: xp[s_, t * 128:(t + 1) * 128, :], lambda t, s_=s_: yp[s_, t * 128:(t + 1) * 128, :],
                   c_rc_p, c_rs_p, NMETA, 0, 16, False)
            if stop == "p2":
                return
        tiles = []
        for j in range(64):
            tiles.append((xs[j * 128:(j + 1) * 128, :], 128, NMETA + j * 128, None, None))
        for j in range(18):
            tiles.append((xso[j * 128:(j + 1) * 128, :], 128, None, j, (c_rc_s, c_rs_s, j * 128)))
        tiles.append((meta[:, :], NMETA, 0, 18, (c_rc_p, c_rs_p, 0)))
        TBt, bTBt = phase1(tiles, (NA_S, NBO_S, NB_S, tbs_bf))
        if stop == "sp1":
            return
        fourier(LS, NA_S, NB_S, NBO_S, fa_s, TBt, bTBt, 64)
        if stop == "sf":
            return
        phase2(lambda t: xso[HALO + t * 128:HALO + (t + 1) * 128, :], lambda t: ys[t * 128:(t + 1) * 128, :],
               c_rc_s, c_rs_s, HALO, 1, 18, True)

    try:
        run_all()
    except _Stop:
        pass
    P.barrier(final=True)

    from contextlib import ExitStack
    with ExitStack() as es:
        sems = {}
        for name in list(Prog.ENGS) + list(P.dcnt.keys()):
            sems[name] = es.enter_context(nc.semaphore("s_" + name))
        block = es.enter_context(nc.Block())

        @block.tensor
        def _(e):
            P.emit("pe", e, sems)

        @block.scalar
        def _(e):
            P.emit("act", e, sems)

        @block.vector
        def _(e):
            P.emit("dve", e, sems)

        @block.gpsimd
        def _(e):
            P.emit("pool", e, sems)

        @block.sync
        def _(e):
            P.emit("sp", e, sems)
    return nc


_NC = None


def kernel(x_prompt, x_sample, meta_tokens, norm_mix_g, w_in, b_gate, attn_sink, w_attn_out, w_fourier, w_out,
           norm_mlp_g, w_mlp_up, w_mlp_down, norm_final_g):
    global _NC
    if _NC is None:
        _NC = build_program()
    nc = _NC
    f = lambda a: np.ascontiguousarray(np.asarray(a, dtype=np.float32))
    x_prompt, x_sample = f(x_prompt), f(x_sample)
    shared = {
        "meta": f(meta_tokens), "w_in": f(w_in[0]), "b_gate": f(np.asarray(b_gate[0]).reshape(16, 128).T),
        "sink": f(np.asarray(attn_sink[0]).reshape(1, 8)), "w_ao": f(w_attn_out[0]), "w_f": f(w_fourier[0]),
        "w_out": f(w_out[0]), "w_up": f(w_mlp_up[0]), "w_dn": f(w_mlp_down[0]),
        "g1": f(np.asarray(norm_mix_g[0]).reshape(1, D)), "g2": f(np.asarray(norm_mlp_g[0]).reshape(1, D)),
        "g3": f(np.asarray(norm_final_g).reshape(1, D)),
    }
    in_maps = []
    for c in range(8):
        r, sq = c % 4, c // 4
        pad = np.zeros((OWN + 2 * HALO, D), np.float32)
        lo, hi = OWN * r - HALO, OWN * r + OWN + HALO
        slo, shi = max(lo, 0), min(hi, DSEQ)
        pad[slo - lo:shi - lo] = x_sample[sq, slo:shi]
        m = dict(shared)
        m["xp"] = np.ascontiguousarray(x_prompt[2 * c:2 * c + 2])
        m["xs"] = np.ascontiguousarray(x_sample[sq])
        m["xso"] = pad
        m.update(_consts(c))
        in_maps.append(m)
    res = run_bass_kernel_spmd(nc, in_maps, core_ids=list(range(8)))
    y_prompt = np.empty((16, SEQ, D), np.float32)
    y_sample = np.empty((2, DSEQ, D), np.float32)
    for c in range(8):
        r, sq = c % 4, c // 4
        y_prompt[2 * c:2 * c + 2] = np.asarray(res.results[c]["yp"]).reshape(2, SEQ, D)
        y_sample[sq, OWN * r:OWN * (r + 1)] = np.asarray(res.results[c]["ys"]).reshape(OWN, D)
    return (y_prompt, y_sample)
```

```python
import numpy as np
import concourse.bass as bass
import concourse.mybir as mybir
from concourse.bass_utils import run_bass_kernel_spmd

F32 = mybir.dt.float32
BF16 = mybir.dt.bfloat16
AF = mybir.ActivationFunctionType
ALU = mybir.AluOpType

D = 1024
SEQ = 2048
DSEQ = 8192
NMETA = 16
LP = SEQ + NMETA
LS = DSEQ + NMETA
DFF = 4096
EPS = 1e-6
THETA = 500000.0
NA_P, NB_P, NBO_P = 24, 86, 86
NA_S, NB_S, NBO_S = 76, 108, 27
OWN = 2048
HALO = 128
SB_BASE = 17408
SB_END = 229376 - 64
NDMASEM = 24


class Buf:
    __slots__ = ("w", "r", "excl")

    def __init__(self, excl=False):
        self.w = None
        self.r = []
        self.excl = excl


class _Rec:
    def __init__(self):
        self.call = None

    def __getattr__(self, name):
        def f(*a, **kw):
            self.call = (name, a, kw)
        return f


def _freeze(fn):
    if fn is None:
        return None
    rec = _Rec()
    fn(rec)
    name, a, kw = rec.call
    return lambda e: getattr(e, name)(*a, **kw)


class Prog:
    ENGS = ("pe", "act", "dve", "pool", "sp")

    def __init__(self):
        self.ops = {e: [] for e in self.ENGS}
        self.cnt = {e: 0 for e in self.ENGS}
        self.seen = {e: {} for e in self.ENGS}
        self.dnames = {"sp": ["d%d" % i for i in range(NDMASEM)], "pool": ["g%d" % i for i in range(NDMASEM)],
                       "bg": ["m%d" % i for i in range(8)]}
        self.dcnt = {n: 0 for e in self.dnames for n in self.dnames[e]}
        self.drr = {"sp": 0, "pool": 0, "bg": 0}

    def _waits(self, eng, deps):
        waits = []
        for (s, c) in sorted(deps):
            if self.seen[eng].get(s, 0) >= c:
                continue
            waits.append((s, c))
            self.seen[eng][s] = c
        return waits

    def _deps(self, eng, r, w):
        deps = set()
        for b in r:
            if b.w is not None:
                deps.add(b.w)
            if b.excl:
                for t in b.r:
                    if t[0] != eng:
                        deps.add(t)
        for b in w:
            if b.w is not None:
                deps.add(b.w)
            for t in b.r:
                if t[0] == eng and eng != "sp":
                    continue
                deps.add(t)
        if eng == "pe":
            deps = {t for t in deps if t[0] != "pe"}
        return deps

    def op(self, eng, fn, r=(), w=(), inc=True):
        deps = self._deps(eng, r, w)
        waits = self._waits(eng, deps)
        if inc:
            self.cnt[eng] += 1
            tok = (eng, self.cnt[eng])
        else:
            tok = (eng, self.cnt[eng] + 1)
        for b in r:
            b.r.append(tok)
        for b in w:
            b.w = tok
            b.r = []
        self.ops[eng].append((waits, _freeze(fn), eng if inc else None, 1))

    def dma(self, eng, fn, r=(), w=(), sempool=None):
        sp_ = sempool or eng
        nm = self.dnames[sp_][self.drr[sp_]]
        self.drr[sp_] = (self.drr[sp_] + 1) % len(self.dnames[sp_])
        deps = self._deps("sp", r, w)
        if self.dcnt[nm] > 0:
            deps.add((nm, self.dcnt[nm]))
        waits = self._waits(eng, deps)
        self.dcnt[nm] += 16
        tok = (nm, self.dcnt[nm])
        for b in r:
            b.r.append(tok)
        for b in w:
            b.w = tok
            b.r = []
        self.ops[eng].append((waits, _freeze(fn), nm, 16))

    def barrier(self, final=False):
        for e in self.ENGS:
            deps = set()
            for x in self.ENGS:
                if x != e and self.cnt[x] > 0:
                    deps.add((x, self.cnt[x]))
            for nm, c in self.dcnt.items():
                if c > 0 and (final or not nm.startswith("m")):
                    deps.add((nm, c))
            waits = self._waits(e, deps)
            if waits:
                self.ops[e].append((waits, None, None, 0))

    def emit(self, eng, e, sems):
        for (waits, fn, incname, incv) in self.ops[eng]:
            for (s, c) in waits:
                e.wait_ge(sems[s], c)
            if fn is None:
                continue
            ins = fn(e)
            if incname is not None:
                ins.then_inc(sems[incname], incv)


class Ring:
    def __init__(self, tiles, bufs=None):
        self.tiles = tiles
        self.bufs = bufs if bufs is not None else [Buf() for _ in tiles]
        self.i = 0

    def next(self):
        t, b = self.tiles[self.i], self.bufs[self.i]
        self.i = (self.i + 1) % len(self.tiles)
        return t, b


class Arena:
    def __init__(self, nc, base, end):
        self.nc, self.base, self.end, self.cur, self.n = nc, base, end, base, 0

    def alloc(self, shape, dtype):
        nbytes = int(np.prod(shape[1:])) * (4 if dtype == F32 else 2)
        nbytes = (nbytes + 63) // 64 * 64
        off = self.cur
        self.cur += nbytes
        assert self.cur <= self.end, ("SBUF arena overflow", self.cur, self.end)
        self.n += 1
        return self.nc.alloc_sbuf_tensor_at("sb%d_%d" % (self.base, self.n), list(shape), dtype, offset=off)

    def sub(self):
        return Arena(self.nc, self.cur, self.end)


def _fourier_tables(L, Na, Nb, Nbo, kstart):
    na = np.arange(Na, dtype=np.int64)[:, None]
    ja = np.arange(Na, dtype=np.int64)[None, :]
    m = (na * (kstart + ja)) % Na
    ang = 2 * np.pi * m / Na
    fa = np.concatenate([np.cos(ang), -np.sin(ang)], axis=1) / np.sqrt(Na)
    nb = np.arange(Nb, dtype=np.int64)[:, None, None]
    kk = kstart + np.arange(Na, dtype=np.int64)[None, :, None] + Na * np.arange(Nbo, dtype=np.int64)[None, None, :]
    phi = 2 * np.pi * ((nb * kk) % L) / L
    tr = np.cos(phi) / np.sqrt(Nb)
    ti = -np.sin(phi) / np.sqrt(Nb)
    tb = np.stack([np.concatenate([tr, ti], axis=2), np.concatenate([-ti, tr], axis=2)], axis=2)
    return fa.astype(np.float32), np.ascontiguousarray(tb.reshape(Nb, -1)).astype(np.float32)


def _rope_tables(pos):
    half = 8
    inv = (np.float32(THETA) ** (-np.arange(half, dtype=np.float32) / np.float32(half))).astype(np.float32)
    ang = (pos.astype(np.float32)[None, :] * inv[:, None]).astype(np.float32)
    c = np.cos(ang).astype(np.float32)
    s = np.sin(ang).astype(np.float32)
    ct = np.concatenate([c, c], axis=0)
    st = np.concatenate([-s, s], axis=0)
    return np.ascontiguousarray(ct), np.ascontiguousarray(st)


def _consts(core):
    r = core % 4
    ident = np.eye(128, dtype=np.float32)
    perm = np.zeros((128, 128), np.float32)
    for m in range(128):
        d = m % 64
        if d < 8:
            perm[m + 8, m] = 1.0
        elif d < 16:
            perm[m - 8, m] = 1.0
    onesh = np.zeros((128, 256), np.float32)
    onesh[:, 0:64] = 1.0
    onesh[:, 192:256] = 1.0
    pj = np.arange(128)[:, None]
    pi = np.arange(128)[None, :]
    mp = (pi <= pj).astype(np.float32)
    mn = (pj <= pi).astype(np.float32)
    mm = np.repeat((np.arange(128)[:, None] < NMETA).astype(np.float32), 128, axis=1)
    masks = np.stack([mp, mn, mp * (1.0 if r > 0 else 0.0), mn * (1.0 if r < 3 else 0.0), mm], axis=1)
    ch = np.arange(128)
    ang = 2 * np.pi * ((ch[:, None] * ch[None, :]) % 128) / 128
    cs = np.concatenate([np.cos(ang), np.sin(ang)], axis=1) / np.sqrt(128.0)
    fa_p, tb_p = _fourier_tables(LP, NA_P, NB_P, NBO_P, NMETA)
    fa_s, tb_s = _fourier_tables(LS, NA_S, NB_S, NBO_S, NMETA + OWN * r)
    cp, sp_ = _rope_tables(np.arange(LP))
    cso, sso = _rope_tables(np.arange(NMETA + OWN * r - HALO, NMETA + OWN * r - HALO + OWN + 2 * HALO))
    return {
        "c_ident": ident, "c_perm": perm, "c_onesh": onesh,
        "c_masks": np.ascontiguousarray(masks.reshape(128, 640)).astype(np.float32),
        "c_cs": cs.astype(np.float32), "c_fa_p": fa_p, "c_tb_p": tb_p, "c_fa_s": fa_s, "c_tb_s": tb_s,
        "c_rc_p": cp, "c_rs_p": sp_, "c_rc_s": cso, "c_rs_s": sso,
    }


class _Stop(Exception):
    pass


def build_program(stop=None):
    nc = bass.Bass("TRN2", target_bir_lowering=False)
    P = Prog()

    def din(name, shape):
        return nc.dram_tensor(name, list(shape), F32, kind="ExternalInput").ap()

    xp = din("xp", [2, SEQ, D])
    xs = din("xs", [DSEQ, D])
    xso = din("xso", [OWN + 2 * HALO, D])
    meta = din("meta", [NMETA, D])
    w_in = din("w_in", [D, 3328])
    b_gate = din("b_gate", [128, 16])
    sinkd = din("sink", [1, 8])
    w_ao = din("w_ao", [512, D])
    w_f = din("w_f", [512, D])
    w_out = din("w_out", [D, D])
    w_up = din("w_up", [D, DFF])
    w_dn = din("w_dn", [DFF, D])
    g1 = din("g1", [1, D])
    g2 = din("g2", [1, D])
    g3 = din("g3", [1, D])
    c_ident = din("c_ident", [128, 128])
    c_perm = din("c_perm", [128, 128])
    c_onesh = din("c_onesh", [128, 256])
    c_masks = din("c_masks", [128, 640])
    c_cs = din("c_cs", [128, 256])
    c_fa_p = din("c_fa_p", [NA_P, 2 * NA_P])
    c_tb_p = din("c_tb_p", [NB_P, NA_P * 4 * NBO_P])
    c_fa_s = din("c_fa_s", [NA_S, 2 * NA_S])
    c_tb_s = din("c_tb_s", [NB_S, NA_S * 4 * NBO_S])
    c_rc_p = din("c_rc_p", [16, LP])
    c_rs_p = din("c_rs_p", [16, LP])
    c_rc_s = din("c_rc_s", [16, OWN + 2 * HALO])
    c_rs_s = din("c_rs_s", [16, OWN + 2 * HALO])
    yp = nc.dram_tensor("yp", [2, SEQ, D], F32, kind="ExternalOutput").ap()
    ys = nc.dram_tensor("ys", [OWN, D], F32, kind="ExternalOutput").ap()
    wup_bf = nc.dram_tensor("wup_bf", [16, 128, 8, 256], BF16).ap()
    wdn_bf = nc.dram_tensor("wdn_bf", [DFF, D], BF16).ap()
    u_scr = nc.dram_tensor("u_scr", [LS, 512], BF16).ap()
    wkvu_bf = nc.dram_tensor("wkvu_bf", [D, 768], BF16).ap()
    tbp_bf = nc.dram_tensor("tbp_bf", [NB_P, NA_P * 4 * NBO_P], BF16).ap()
    tbs_bf = nc.dram_tensor("tbs_bf", [NB_S, NA_S * 4 * NBO_S], BF16).ap()

    A = Arena(nc, SB_BASE, SB_END)
    Wq = A.alloc([128, 8, 4, 128], BF16); bWq = []
    Wg = A.alloc([128, 8, 2048], BF16); bWg = []
    Wao = A.alloc([128, 4, D], BF16); bWao = []
    Wf = A.alloc([128, 4, D], BF16); bWf = []
    Wout = A.alloc([128, 8, D], BF16); bWout = []
    g1b = A.alloc([128, D], F32); g2b = A.alloc([128, D], F32); g3b = A.alloc([128, D], F32)
    bG = [Buf(), Buf(), Buf()]
    ident = A.alloc([128, 128], BF16); perm = A.alloc([128, 128], BF16)
    onesh = A.alloc([128, 256], BF16); masks = A.alloc([128, 5, 128], BF16)
    sinkb = A.alloc([128, 4, 128], F32); sinke = A.alloc([128, 4], F32); zeros = A.alloc([128, 128], F32)
    bg = A.alloc([128, 16], F32)
    cs = A.alloc([128, 256], BF16)
    fa_p = A.alloc([128, 2 * NA_P], BF16); fa_s = A.alloc([128, 2 * NA_S], BF16)
    bC = []
    ctab = A.alloc([128, 512], F32); stab = A.alloc([128, 512], F32); bTab = [Buf() for _ in range(4)]
    stat_t = A.alloc([128, 32], F32)
    dummy_t = A.alloc([128, 2], F32)
    fT = A.alloc([128, 4, NBO_P * NA_P], BF16); bfT = [Buf() for _ in range(4)]
    kT = A.alloc([128, 19 * 128], BF16); bkT = Buf()
    Vp = A.alloc([128, 19, 192], BF16); bVp = Buf()
    R = A.sub()

    psr = Ring([nc.alloc_psum_tensor("ps%d" % i, [128, 512], F32) for i in range(7)], [Buf(excl=True) for _ in range(7)])
    psA = Ring(psr.tiles[0:4], psr.bufs[0:4])
    psB = Ring(psr.tiles[4:7], psr.bufs[4:7])
    pst_t = nc.alloc_psum_tensor("pst", [128, 8, 128], BF16)
    bpst = Buf(excl=True)
    stat_i = [0]
    stat_b = [Buf() for _ in range(32)]

    def stat():
        i = stat_i[0]
        stat_i[0] = (i + 1) % 32
        return stat_t[:, i:i + 1], stat_b[i]

    def bcast_g(t):
        return t

    def cdma(out_ap, in_ap, r=(), w=(), sempool=None):
        P.dma("pool", lambda e, o=out_ap, i=in_ap: e.dma_start(out=o, in_=i), r, w, sempool=sempool)

    def sdma(out_ap, in_ap, r=(), w=()):
        P.dma("sp", lambda e, o=out_ap, i=in_ap: e.dma_start(out=o, in_=i), r, w)

    def nb(lst):
        b_ = Buf()
        lst.append(b_)
        return [b_]

    cdma(ident[:], c_ident[:, :], w=nb(bC))
    cdma(perm[:], c_perm[:, :], w=nb(bC))
    cdma(onesh[:], c_onesh[:, :], w=nb(bC))
    cdma(masks[:].rearrange("p a b -> p (a b)"), c_masks[:, :], w=nb(bC))
    cdma(cs[:], c_cs[:, :], w=nb(bC))
    cdma(fa_p[0:NA_P, :], c_fa_p[:, :], w=nb(bC))
    cdma(fa_s[0:NA_S, :], c_fa_s[:, :], w=nb(bC))
    sdma(bg[:], b_gate[:, :], w=nb(bC))
    sdma(g1b[:], g1[0:1, :].partition_broadcast(128), w=[bG[0]])
    sdma(g2b[:], g2[0:1, :].partition_broadcast(128), w=[bG[1]])
    sdma(g3b[:], g3[0:1, :].partition_broadcast(128), w=[bG[2]])
    bScr = []
    cdma(wkvu_bf[:, :], w_in[:, 512:1280], w=nb(bScr))
    cdma(tbp_bf[:, :].rearrange("p (x y) -> p x y", x=8), c_tb_p[:, :].rearrange("p (x y) -> p x y", x=8), w=nb(bScr))
    cdma(tbs_bf[:, :].rearrange("p (x y) -> p x y", x=8), c_tb_s[:, :].rearrange("p (x y) -> p x y", x=8), w=nb(bScr))
    bsk = [Buf(), Buf()]
    sdma(sinke[0:64, :], sinkd[0:1, 0:4].partition_broadcast(64), w=[bsk[0]])
    sdma(sinke[64:128, :], sinkd[0:1, 4:8].partition_broadcast(64), w=[bsk[1]])
    bz = nb(bC)
    P.op("pool", lambda e: e.memset(zeros[:], 0.0), w=bz)
    P.op("pool", lambda e: e.memset(ctab[:], 1.0), w=bTab)
    P.op("pool", lambda e: e.memset(stab[:], 0.0), w=bTab)
    P.op("pool", lambda e: e.memset(Vp[:].rearrange("p a b -> p (a b)"), 0.0), w=[bVp])
    P.op("pool", lambda e: e.memset(kT[:, 18 * 128:19 * 128], 0.0), w=[bkT])
    for gi, gt in enumerate((g1b, g2b, g3b)):
        P.op("dve", lambda e, gt=gt: e.tensor_scalar(out=gt[:], in0=gt[:], scalar1=float(np.sqrt(D)), scalar2=None,
                                                      op0=ALU.mult), r=[bG[gi]], w=[bG[gi]])
    P.op("act", lambda e: e.activation(out=sinke[:], in_=sinke[:], func=AF.Exp), r=bsk, w=bsk)
    bsb = nb(bC)
    for g in range(4):
        P.op("dve", lambda e, g=g: e.tensor_scalar(out=sinkb[:, g, :], in0=zeros[:], scalar1=sinke[:, g:g + 1],
                                                    scalar2=None, op0=ALU.add), r=bsk + bz, w=bsb)
    bWupD, bWdnD = [], []

    def setup_late():
        w_in_v = w_in.rearrange("(kt p) c -> p kt c", p=128)
        for g in range(4):
            for hh in range(2):
                c0 = (hh * 4 + g) * 64
                cdma(Wq[:, :, g, hh * 64:(hh + 1) * 64], w_in_v[:, :, c0:c0 + 64], w=nb(bWq))
        for kt in range(8):
            cdma(Wg[:, kt, :], w_in[kt * 128:(kt + 1) * 128, 1280:3328], w=nb(bWg))
        for g in range(4):
            for hh in range(2):
                r0 = (hh * 4 + g) * 64
                cdma(Wao[hh * 64:(hh + 1) * 64, g, :], w_ao[r0:r0 + 64, :], w=nb(bWao))
        for gr in range(4):
            cdma(Wf[:, gr, :], w_f[gr * 128:(gr + 1) * 128, :], w=nb(bWf))
        for kt in range(8):
            cdma(Wout[:, kt, :], w_out[kt * 128:(kt + 1) * 128, :], w=nb(bWout))
        for i in range(8):
            cdma(wup_bf[:, :, i, :].rearrange("c p b -> p c b"),
                 w_up[i * 128:(i + 1) * 128, :].rearrange("p (c b) -> p c b", b=256), w=nb(bWupD), sempool="bg")
        for i in range(32):
            cdma(wdn_bf[i * 128:(i + 1) * 128, :], w_dn[i * 128:(i + 1) * 128, :], w=nb(bWdnD), sempool="bg")


    def rms_scale(x_ap, bx, n, gb_t, out_ap, bout, junk_ap, bjunk):
        ssq, bs = stat()
        sq, bq = stat()
        rs, br = stat()
        bxl = bx if isinstance(bx, list) else [bx]
        P.op("act", lambda e: e.activation(out=junk_ap, in_=x_ap, func=AF.Square, accum_out=ssq[0:n, :]),
             r=bxl, w=[bjunk, bs])
        P.op("act", lambda e: e.activation(out=sq[0:n, :], in_=ssq[0:n, :], func=AF.Sqrt, bias=float(D * EPS)),
             r=[bs], w=[bq])
        P.op("dve", lambda e: e.reciprocal(out=rs[0:n, :], in_=sq[0:n, :]), r=[bq], w=[br])
        P.op("dve", lambda e: e.scalar_tensor_tensor(out=out_ap, in0=x_ap, scalar=rs[0:n, :], in1=gb_t[0:n, :],
                                                     op0=ALU.mult, op1=ALU.mult), r=bxl + [br] + bG, w=[bout])

    def transposes(src_t, bsrc, n, dst_ap, bdst, evac_eng):
        for kt in range(8):
            P.op("pe", lambda e, kt=kt: e.transpose(out=pst_t[:, kt, 0:n], in_=src_t[0:n, kt * 128:(kt + 1) * 128],
                                                     identity=ident[0:n, 0:n]),
                 r=[bsrc] + bC, w=[bpst], inc=(kt == 7))
        if evac_eng == "act":
            P.op("act", lambda e: e.activation(out=dst_ap, in_=pst_t[:, :, 0:n], func=AF.Copy), r=[bpst], w=[bdst])
        else:
            P.op("dve", lambda e: e.tensor_copy(out=dst_ap, in_=pst_t[:, :, 0:n]), r=[bpst], w=[bdst])

    def load_rope(rc, rs, p0, n):
        i = 0
        for (tab, src) in ((ctab, rc), (stab, rs)):
            for base in (0, 64):
                sdma(tab[base:base + 16, 0:n], src[:, p0:p0 + n], w=[bTab[i]])
                i += 1

    def rope(ps_raw, braw, n, tabs, out_ap, bout, tmpr):
        rawb, brb = tmpr["rawb"].next()
        t1, b1 = tmpr["f32"].next()
        t2, b2 = tmpr["f32"].next()
        ps2, bp2 = psr.next()
        P.op("act", lambda e: e.activation(out=rawb[:, 0:n], in_=ps_raw, func=AF.Copy), r=[braw], w=[brb])
        P.op("pe", lambda e: e.matmul(ps2[:, 0:n], lhsT=perm[:], rhs=rawb[:, 0:n], start=True, stop=True),
             r=[brb] + bC, w=[bp2])
        if stop == 'p2b2' and n == 512:
            raise _Stop()
        ct_ap, st_ap, btabs = tabs
        P.op("dve", lambda e: e.tensor_tensor(out=t1[:, 0:n], in0=ps_raw, in1=ct_ap, op=ALU.mult),
             r=[braw] + btabs, w=[b1])
        P.op("dve", lambda e: e.tensor_tensor(out=t2[:, 0:n], in0=ps2[:, 0:n], in1=st_ap, op=ALU.mult),
             r=[bp2] + btabs, w=[b2])
        if stop == 'p2b3' and n == 512:
            raise _Stop()
        if isinstance(out_ap, tuple):
            qp, g_ = out_ap
            P.op("pool", lambda e: e.tensor_tensor(out=qp[0][0:64, g_, :], in0=t1[0:64, 0:n], in1=t2[0:64, 0:n], op=ALU.add),
                 r=[b1, b2], w=[bout[0]])
            P.op("dve", lambda e: e.tensor_tensor(out=qp[1][64:128, g_, :], in0=t1[64:128, 0:n], in1=t2[64:128, 0:n], op=ALU.add),
                 r=[b1, b2], w=[bout[1]])
        else:
            P.op("pool", lambda e: e.tensor_tensor(out=out_ap, in0=t1[:, 0:n], in1=t2[:, 0:n], op=ALU.add),
                 r=[b1, b2], w=[bout])

    def phase1(tiles, tb_spec):
        P.barrier()
        a = R.sub()
        (Na_, Nbo_, Nb_, c_tb_) = tb_spec
        tb_bytes = Na_ * 2 * 2 * Nbo_ * 2
        TB_ = nc.alloc_sbuf_tensor_at("tb%d" % P.cnt["pe"], [128, Na_, 2, 2 * Nbo_], BF16, offset=SB_END - tb_bytes)
        bTB_ = Buf()
        wtab = sum(t[1] for t in tiles if t[3] is not None)
        ctL = a.alloc([128, wtab], F32); stL = a.alloc([128, wtab], F32)
        btL = [Buf(), Buf()]
        P.op("pool", lambda e: e.memset(ctL[:], 1.0), w=[btL[0]])
        P.op("pool", lambda e: e.memset(stL[:], 0.0), w=[btL[1]])
        runs = []
        col = 0
        tabcol = {}
        for ti, t in enumerate(tiles):
            if t[3] is None:
                continue
            rc_, rs_, p0_ = t[4]
            tabcol[ti] = col
            if runs and runs[-1][0] is rc_ and runs[-1][2] + runs[-1][4] == p0_:
                runs[-1][4] += t[1]
            else:
                runs.append([rc_, rs_, p0_, col, t[1]])
            col += t[1]
        Wk = a.alloc([128, 8, 128], BF16); Wv = a.alloc([128, 8, 128], BF16); Wu = a.alloc([128, 8, 512], BF16)
        bWk, bWv, bWu = [], [], []
        xst = Ring([a.alloc([128, D], F32) for _ in range(4)])
        junk = a.alloc([128, D], BF16); bjunk = Buf()
        xnr = Ring([a.alloc([128, D], BF16) for _ in range(3)])
        xnTr = Ring([a.alloc([128, 8, 128], BF16) for _ in range(3)])
        ustr = Ring([a.alloc([128, 512], BF16) for _ in range(3)])
        tmpr = {"rawb": Ring([a.alloc([128, 128], BF16) for _ in range(2)]),
                "f32": Ring([a.alloc([128, 128], F32) for _ in range(4)])}
        btabs = []

        def emit_loads():
            wkvu_v = wkvu_bf.rearrange("(kt p) c -> p kt c", p=128)
            sdma(Wu[:, :, :], wkvu_v[:, :, 256:768], r=bScr, w=nb(bWu))
            sdma(Wv[:, :, :], wkvu_v[:, :, 128:256], r=bScr, w=nb(bWv))
            sdma(Wk[:, :, :], wkvu_v[:, :, 0:128], r=bScr, w=nb(bWk))
            for (rc_, rs_, p0_, c0_, n_) in runs:
                for (tab_, src_, bi_) in ((ctL, rc_, 0), (stL, rs_, 1)):
                    for base in (0, 64):
                        sdma(tab_[base:base + 16, c0_:c0_ + n_], src_[:, p0_:p0_ + n_], r=[btL[bi_]], w=nb(btabs))

        def emit_tb():
            sdma(TB_[0:Nb_].rearrange("p a b c -> p (a b c)"), c_tb_[:, :], r=bScr, w=[bTB_])
        def stage1(tile):
            (src, n, urow, kvb, ropei, ti) = tile
            xt, bx = xst.next()
            sdma(xt[0:n, :], src, w=[bx])
            xn, bxn = xnr.next()
            rms_scale(xt[0:n, :], bx, n, g1b, xn[0:n, :], bxn, junk[0:n, :], bjunk)
            return (tile, xn, bxn)

        deferred = []

        def run_deferred():
            while deferred:
                deferred.pop(0)()

        def stage2(tile, xn, bxn):
            (src, n, urow, kvb, ropei, ti) = tile
            xnT, bxT = xnTr.next()
            transposes(xn, bxn, n, xnT[:, :, 0:n], bxT, "act")
            if urow is not None:
                ps, bp = psr.next()
                for kt in range(8):
                    P.op("pe", lambda e: e.matmul(ps[0:n, :], lhsT=xnT[:, kt, 0:n], rhs=Wu[:, kt, :],
                                                  start=(kt == 0), stop=(kt == 7)),
                         r=[bxT] + bWu, w=[bp], inc=(kt == 7))
                us, bus = ustr.next()
                P.op("dve", lambda e: e.tensor_copy(out=us[0:n, :], in_=ps[0:n, :]), r=[bp], w=[bus])
                cdma(u_scr[urow:urow + n, :], us[0:n, :], r=[bus], w=[Buf()])
            if kvb is not None:
                ps, bp = psr.next()
                for kt in range(8):
                    P.op("pe", lambda e: e.matmul(ps[0:n, 0:128], lhsT=xnT[:, kt, 0:n], rhs=Wv[:, kt, :],
                                                  start=(kt == 0), stop=(kt == 7)),
                         r=[bxT] + bWv, w=[bp], inc=(kt == 7))
                P.op("act", lambda e: e.activation(out=Vp[0:n, kvb, 0:64], in_=ps[0:n, 0:64], func=AF.Copy),
                     r=[bp], w=[bVp])
                P.op("act", lambda e: e.activation(out=Vp[0:n, kvb, 128:192], in_=ps[0:n, 64:128], func=AF.Copy),
                     r=[bp], w=[bVp])
                psk, bpk = psr.next()
                for kt in range(8):
                    P.op("pe", lambda e: e.matmul(psk[:, 0:n], lhsT=Wk[:, kt, :], rhs=xnT[:, kt, 0:n],
                                                  start=(kt == 0), stop=(kt == 7)),
                         r=[bxT] + bWk, w=[bpk], inc=(kt == 7))
                c0 = tabcol[ti]
                run_deferred()
                deferred.append(lambda: rope(psk[:, 0:n], bpk, n, (ctL[:, c0:c0 + n], stL[:, c0:c0 + n], btabs),
                                             kT[:, kvb * 128:kvb * 128 + n], bkT, tmpr))
            else:
                run_deferred()

        pend = []
        for ti, tile in enumerate(tiles):
            pend.append(stage1(tuple(tile) + (ti,)))
            if ti == min(2, len(tiles) - 1):
                emit_loads()
            if ti == min(6, len(tiles) - 1):
                emit_tb()
            if len(pend) > 2:
                stage2(*pend.pop(0))
        while pend:
            stage2(*pend.pop(0))
        run_deferred()
        return TB_, bTB_

    late_pending = [True]

    def fourier(L, Na, Nb, Nbo, fa_t, TB, bTB, CW):
        P.barrier()
        if late_pending[0]:
            late_pending[0] = False
            setup_late()
        a = R.sub()
        nblk = 128 // CW
        ugr = Ring([a.alloc([128, Nb, CW], BF16) for _ in range(2)])
        Yh = a.alloc([128, CW, 2 * Na], BF16); bYh = Buf()
        Vg = a.alloc([128, 2, Na, Nbo], BF16); bVg = Buf()
        ev = [0]

        def evac(out_ap, in_ap, r, w):
            ev[0] += 1
            if ev[0] % 2:
                P.op("act", lambda e: e.activation(out=out_ap, in_=in_ap, func=AF.Copy), r=r, w=w)
            else:
                P.op("dve", lambda e: e.tensor_copy(out=out_ap, in_=in_ap), r=r, w=w)

        u_v = u_scr[0:L, :].rearrange("(na nb) c -> na nb c", nb=Nb)
        nA = 512 // (2 * Na)
        nB = 512 // (2 * Nbo)
        slots = {}

        def load_u(i):
            if i >= 4 * nblk or i in slots:
                return
            g_, hf_ = i // nblk, i % nblk
            t_, b_ = ugr.next()
            slots[i] = (t_, b_)
            sdma(t_[0:Na, :, :], u_v[:, :, g_ * 128 + hf_ * CW:g_ * 128 + hf_ * CW + CW], w=[b_])

        def stage_a(g, hf):
            load_u(g * nblk + hf + 1)
            Ug, bUg = slots[g * nblk + hf]
            c = 0
            while c < CW:
                nch = min(nA, CW - c)
                ps, bp = psr.next()
                for j in range(nch):
                    P.op("pe", lambda e: e.matmul(ps[0:Nb, j * 2 * Na:(j + 1) * 2 * Na],
                                                  lhsT=Ug[0:Na, :, c + j], rhs=fa_t[0:Na, :], start=True, stop=True),
                         r=[bUg] + bC, w=[bp], inc=(j == nch - 1))
                evac(Yh[0:Nb, c:c + nch, :], ps[0:Nb, 0:nch * 2 * Na].rearrange("p (a b) -> p a b", b=2 * Na),
                     [bp], [bYh])
                c += nch

        def stage_b(g, hf):
            ja = 0
            while ja < Na:
                nj = min(nB, Na - ja)
                ps, bp = psr.next()
                for j in range(nj):
                    for cc in range(2):
                        P.op("pe", lambda e: e.matmul(
                            ps[hf * CW:(hf + 1) * CW, j * 2 * Nbo:(j + 1) * 2 * Nbo], lhsT=Yh[0:Nb, :, cc * Na + ja + j],
                            rhs=TB[0:Nb, ja + j, cc, :], start=(cc == 0), stop=(cc == 1)),
                             r=[bYh, bTB], w=[bp], inc=(j == nj - 1 and cc == 1))
                evac(Vg[hf * CW:(hf + 1) * CW, :, ja:ja + nj, :].rearrange("p c a b -> p a c b"),
                     ps[hf * CW:(hf + 1) * CW, 0:nj * 2 * Nbo].rearrange("p (a c b) -> p a c b", c=2, b=Nbo), [bp], [bVg])
                ja += nj

        def stage_c(g):
            ca = 512 // Nbo
            fTg = fT[:, g, 0:Nbo * Na].rearrange("p (b a) -> p b a", a=Na)
            ja = 0
            while ja < Na:
                k = min(ca, Na - ja)
                ps, bp = psr.next()
                P.op("pe", lambda e: e.matmul(ps[:, 0:k * Nbo], lhsT=cs[:, 0:128],
                                              rhs=Vg[:, 0, ja:ja + k, :].rearrange("p a b -> p (a b)"),
                                              start=True, stop=False), r=[bVg] + bC, w=[bp], inc=False)
                P.op("pe", lambda e: e.matmul(ps[:, 0:k * Nbo], lhsT=cs[:, 128:256],
                                              rhs=Vg[:, 1, ja:ja + k, :].rearrange("p a b -> p (a b)"),
                                              start=False, stop=True), r=[bVg] + bC, w=[bp])
                evac(fTg[:, :, ja:ja + k].rearrange("p b a -> p a b"),
                     ps[:, 0:k * Nbo].rearrange("p (a b) -> p a b", b=Nbo), [bp], [bfT[g]])
                ja += k

        load_u(0)
        for g in range(4):
            for hf in range(nblk):
                stage_a(g, hf)
                if hf == 0 and g > 0:
                    stage_c(g - 1)
                stage_b(g, hf)
        stage_c(3)

    def phase2(x_src, y_dst, rc, rs, pos_base, kvoff, nkb, edge_masks):
        P.barrier()
        a = R.sub()
        xh = a.alloc([128, 4, D], F32); bxh = [Buf() for _ in range(4)]
        xnT = a.alloc([128, 8, 512], BF16); bxnT = [Buf() for _ in range(4)]
        wupr = Ring([a.alloc([128, 8, 256], BF16) for _ in range(2)])
        wdnr = Ring([a.alloc([128, 4, D], BF16) for _ in range(2)])
        relu_off = a.cur
        relur = Ring([a.alloc([128, 512], F32) for _ in range(2)])
        actT = a.alloc([128, 32, 512], BF16); bact = [Buf() for _ in range(32)]
        b = Arena(nc, a.cur - 32 * 512 * 2, a.cur)
        xnr = Ring([b.alloc([128, D], BF16) for _ in range(2)])
        qTp = [b.alloc([128, 4, 512], BF16) for _ in range(2)]; bq = [Buf() for _ in range(8)]
        aT = b.alloc([128, 4, 512], BF16); baT = [Buf() for _ in range(4)]
        mT = b.alloc([128, 8, 512], BF16); bm = [Buf() for _ in range(8)]
        ptr = Ring([b.alloc([128, 512], BF16) for _ in range(3)])
        tmpr = {"rawb": ptr,
                "f32": None}
        f32_own = Ring([b.alloc([128, 512], F32) for _ in range(2)])
        tmpr["f32"] = Ring(f32_own.tiles + relur.tiles, f32_own.bufs + relur.bufs)
        xstage = nc.alloc_sbuf_tensor_at("xstage%d" % a.base, [128, D], F32, offset=relu_off)
        bxs = relur.bufs
        alias_all = bact + bq + baT + bm + ptr.bufs + tmpr["f32"].bufs + xnr.bufs

        class Stream:
            def __init__(self, ring, total, loader):
                self.ring, self.total, self.loader, self.loaded, self.slots = ring, total, loader, 0, {}

            def ensure(self, j):
                while self.loaded < min(j + 2, self.total):
                    t, bb = self.ring.next()
                    self.slots[self.loaded] = (t, bb)
                    self.loader(self.loaded, t, bb)
                    self.loaded += 1

            def get(self, j):
                self.ensure(j)
                return self.slots[j]

        def load_up(j, t, bb):
            i = j % 16
            sdma(t[:], wup_bf[i], r=bWupD, w=[bb])

        def load_dn(j, t, bb):
            i = j % 8
            sdma(t[:], wdn_bf[i * 512:(i + 1) * 512, :].rearrange("(f p) c -> p f c", p=128), r=bWdnD, w=[bb])

        sup = Stream(wupr, 4 * 16, load_up)
        sdn = Stream(wdnr, 4 * 8, load_dn)

        def final_norm(st):
            for t in range(4):
                ssq, bs = stat()
                sq, bq2 = stat()
                rs_, br = stat()
                jk, bjk = xnr.next()
                P.op("act", lambda e: e.activation(out=jk[:, :], in_=xh[:, t, :], func=AF.Square, accum_out=ssq),
                     r=[bxh[t]], w=[bjk, bs])
                P.op("act", lambda e: e.activation(out=sq, in_=ssq, func=AF.Sqrt, bias=float(D * EPS)),
                     r=[bs], w=[bq2])
                P.op("dve", lambda e: e.reciprocal(out=rs_, in_=sq), r=[bq2], w=[br])
                P.op("dve", lambda e: e.scalar_tensor_tensor(out=xh[:, t, :], in0=xh[:, t, :], scalar=rs_,
                                                             in1=g3b[:], op0=ALU.mult, op1=ALU.mult),
                     r=[bxh[t], br] + bG, w=[bxh[t]])
                sdma(y_dst(st * 4 + t), xh[:, t, :], r=[bxh[t]])
            if st < 3:
                for t in range(4):
                    sdma(xh[:, t, :], x_src((st + 1) * 4 + t), w=[bxh[t]])

        for st in range(4):
            if st == 0:
                for t in range(4):
                    sdma(xh[:, t, :], x_src(st * 4 + t), w=[bxh[t]])
                pend = []
                for t in range(5):
                    if t < 4:
                        xn, bxn = xnr.next()
                        rms_scale(xh[:, t, :], bxh[t], 128, g1b, xn[:, :], bxn, xn[:, :], bxn)
                        pend.append((t, xn, bxn))
                    if t >= 1:
                        (t2, xn2, bxn2) = pend.pop(0)
                        transposes(xn2, bxn2, 128, xnT[:, :, t2 * 128:(t2 + 1) * 128], bxnT[t2], "dve" if t2 % 2 else "act")
                load_rope(rc, rs, pos_base + st * 512, 512)
            if stop == 'p2a':
                raise _Stop()
            P.op("pool", lambda e: e.memset(qTp[0][64:128].rearrange("p a b -> p (a b)"), 0.0), w=bq[0:4])
            P.op("pool", lambda e: e.memset(qTp[1][0:64].rearrange("p a b -> p (a b)"), 0.0), w=bq[4:8])
            qps = []
            for g in range(4):
                ps, bp = psr.next()
                for kt in range(8):
                    P.op("pe", lambda e: e.matmul(ps[:, :], lhsT=Wq[:, kt, g, :], rhs=xnT[:, kt, :],
                                                  start=(kt == 0), stop=(kt == 7)),
                         r=bxnT + bWq, w=[bp], inc=(kt == 7))
                qps.append((ps, bp))
            for g in range(4):
                ps, bp = qps[g]
                rope(ps[:, :], bp, 512, (ctab[:, 0:512], stab[:, 0:512], bTab), (qTp, g), (bq[g], bq[4 + g]), tmpr)
            if st >= 1:
                final_norm(st - 1)
            if stop == 'p2b':
                raise _Stop()
            items = []
            for qb in range(4):
                gq = st * 4 + qb
                kb_c = gq + kvoff
                blocks = [(18, 128, 4)]
                if kb_c - 1 >= 0:
                    blocks.append((kb_c - 1, 128, (2 if (edge_masks and gq == 0) else 0)))
                blocks.append((kb_c, 128, None))
                if kb_c + 1 < nkb:
                    blocks.append((kb_c + 1, 128, (3 if (edge_masks and gq == 15) else 1)))
                n_it = 2 * len(blocks)
                k_it = 0
                for h in range(2):
                    for (kvb, nk, mi) in blocks:
                        items.append(dict(qb=qb, h=h, kvb=kvb, nk=nk, mi=mi, first=(k_it == 0), last=(k_it == n_it - 1)))
                        k_it += 1

            def score(it):
                ps, bp = psB.next()
                it["ps"], it["bp"] = ps, bp
                h, nk, qb = it["h"], it["nk"], it["qb"]
                kcol = it["kvb"] * 128
                P.op("pe", lambda e: e.matmul(ps[0:nk, :], lhsT=kT[:, kcol:kcol + nk],
                                              rhs=qTp[h][:, :, qb * 128:(qb + 1) * 128], start=True, stop=True),
                     r=[bkT] + bq, w=[bp])

            LOOK = 2
            for i in range(min(LOOK, len(items))):
                score(items[i])
            acc = None
            for i, it in enumerate(items):
                if i + LOOK < len(items):
                    score(items[i + LOOK])
                if it["first"]:
                    pnum, bnum = psA.next()
                    pden, bden = psA.next()
                    acc = (pnum, bnum, pden, bden)
                pnum, bnum, pden, bden = acc
                ps, bp, nk, h, kvb, mi, qb = it["ps"], it["bp"], it["nk"], it["h"], it["kvb"], it["mi"], it["qb"]
                pt, bpt = ptr.next()
                P.op("act", lambda e: e.activation(out=pt[0:nk, :], in_=ps[0:nk, :], func=AF.Exp, scale=0.125), r=[bp], w=[bpt])
                if mi is not None:
                    mk = bass.AP(masks, mi * 128, [[640, 128], [0, 4], [1, 128]])
                    P.op("pool", lambda e: e.tensor_tensor(
                        out=pt[:, :].rearrange("p (g q) -> p g q", g=4), in0=pt[:, :].rearrange("p (g q) -> p g q", g=4),
                        in1=mk, op=ALU.mult), r=[bpt] + bC, w=[bpt])
                first, last = it["first"], it["last"]
                P.op("pe", lambda e: e.matmul(pnum[:, :], lhsT=Vp[0:nk, kvb, h * 64:h * 64 + 128], rhs=pt[0:nk, :],
                                              start=first, stop=last), r=[bpt, bVp], w=[bnum], inc=last)
                P.op("pe", lambda e: e.matmul(pden[:, :], lhsT=onesh[0:nk, h * 128:(h + 1) * 128], rhs=pt[0:nk, :],
                                              start=first, stop=last), r=[bpt] + bC, w=[bden], inc=True)
                if last:
                    t1, b1 = tmpr["f32"].next()
                    P.op("dve", lambda e: e.tensor_tensor(out=t1[:, :], in0=pden[:, :],
                                                          in1=sinkb[:].rearrange("p g q -> p (g q)"), op=ALU.add),
                         r=[bden] + bC, w=[b1])
                    P.op("dve", lambda e: e.reciprocal(out=t1[:, :], in_=t1[:, :]), r=[b1], w=[b1])
                    P.op("dve", lambda e: e.tensor_tensor(
                        out=aT[:, :, qb * 128:(qb + 1) * 128], in0=pnum[:, :].rearrange("p (g q) -> p g q", g=4),
                        in1=t1[:, :].rearrange("p (g q) -> p g q", g=4), op=ALU.mult), r=[bnum, b1], w=[baT[qb]])
            if stop == 'p2c':
                raise _Stop()
            for c in range(8):
                pga, bga = psA.next()
                pgf, bgf = psA.next()
                pa, bpa = psB.next()
                pf, bpf = psB.next()
                for (pg, bgx, off) in ((pga, bga, 0), (pgf, bgf, 1024)):
                    for kt in range(8):
                        P.op("pe", lambda e, kt=kt, pg=pg, off=off, c=c: e.matmul(
                            pg[:, :], lhsT=Wg[:, kt, off + c * 128:off + (c + 1) * 128], rhs=xnT[:, kt, :],
                            start=(kt == 0), stop=(kt == 7)), r=bxnT + bWg, w=[bgx], inc=(kt == 7))
                for g in range(4):
                    P.op("pe", lambda e, g=g, c=c, pa=pa: e.matmul(pa[:, :], lhsT=Wao[:, g, c * 128:(c + 1) * 128], rhs=aT[:, g, :],
                                                                    start=(g == 0), stop=(g == 3)),
                         r=baT + bWao, w=[bpa], inc=(g == 3))
                for g in range(4):
                    P.op("pe", lambda e, g=g, c=c, pf=pf: e.matmul(pf[:, :], lhsT=Wf[:, g, c * 128:(c + 1) * 128],
                                                                    rhs=fT[:, g, st * 512:(st + 1) * 512],
                                                                    start=(g == 0), stop=(g == 3)),
                         r=bfT + bWf, w=[bpf], inc=(g == 3))
                sa, bsa = tmpr["f32"].next()
                sf, bsf = tmpr["f32"].next()
                P.op("act", lambda e, pga=pga, sa=sa, c=c: e.activation(out=sa[:, :], in_=pga[:, :], func=AF.Sigmoid,
                                                                        bias=bg[:, c:c + 1]), r=[bga] + bC, w=[bsa])
                P.op("act", lambda e, pgf=pgf, sf=sf, c=c: e.activation(out=sf[:, :], in_=pgf[:, :], func=AF.Sigmoid,
                                                                        bias=bg[:, 8 + c:9 + c]), r=[bgf] + bC, w=[bsf])
                P.op("dve", lambda e, pa=pa, sa=sa: e.tensor_tensor(out=sa[:, :], in0=pa[:, :], in1=sa[:, :], op=ALU.mult),
                     r=[bpa, bsa], w=[bsa])
                P.op("dve", lambda e, pf=pf, sf=sf: e.tensor_tensor(out=sf[:, :], in0=pf[:, :], in1=sf[:, :], op=ALU.mult),
                     r=[bpf, bsf], w=[bsf])
                P.op("pool", lambda e, sa=sa, sf=sf, c=c: e.tensor_tensor(out=mT[:, c, :], in0=sa[:, :], in1=sf[:, :], op=ALU.add),
                     r=[bsa, bsf], w=[bm[c]])
            if stop == 'p2d':
                raise _Stop()
            for t in range(4):
                for hf in range(2):
                    ps, bp = psr.next()
                    for c in range(8):
                        P.op("pe", lambda e, c=c, t=t, hf=hf, ps=ps: e.matmul(
                            ps[:, :], lhsT=mT[:, c, t * 128:(t + 1) * 128], rhs=Wout[:, c, hf * 512:(hf + 1) * 512],
                            start=(c == 0), stop=(c == 7)), r=bm + bWout, w=[bp], inc=(c == 7))
                    P.op("dve", lambda e, t=t, hf=hf, ps=ps: e.tensor_tensor(
                        out=xh[:, t, hf * 512:(hf + 1) * 512], in0=ps[:, :], in1=xh[:, t, hf * 512:(hf + 1) * 512], op=ALU.add),
                         r=[bp, bxh[t]], w=[bxh[t]])
            if stop == 'p2e':
                raise _Stop()
            sup.ensure(st * 16)
            pend = []
            for t in range(5):
                if t < 4:
                    xn, bxn = xnr.next()
                    rms_scale(xh[:, t, :], bxh[t], 128, g2b, xn[:, :], bxn, xn[:, :], bxn)
                    pend.append((t, xn, bxn))
                if t >= 1:
                    (t2, xn2, bxn2) = pend.pop(0)
                    transposes(xn2, bxn2, 128, xnT[:, :, t2 * 128:(t2 + 1) * 128], bxnT[t2], "dve" if t2 % 2 else "act")
            sdn.ensure(st * 8)
            for i in range(16):
                wt, bw = sup.get(st * 16 + i)
                for f in range(2):
                    ffc = i * 2 + f
                    ps, bp = psr.next()
                    for kt in range(8):
                        P.op("pe", lambda e, kt=kt, f=f, wt=wt, ps=ps: e.matmul(
                            ps[:, :], lhsT=wt[:, kt, f * 128:(f + 1) * 128], rhs=xnT[:, kt, :],
                            start=(kt == 0), stop=(kt == 7)), r=bxnT + [bw], w=[bp], inc=(kt == 7))
                    rl, brl = relur.next()
                    P.op("act", lambda e, ps=ps, rl=rl: e.activation(out=rl[:, :], in_=ps[:, :], func=AF.Relu), r=[bp], w=[brl])
                    P.op("dve", lambda e, rl=rl, ffc=ffc: e.tensor_tensor(out=actT[:, ffc, :], in0=rl[:, :], in1=rl[:, :],
                                                                           op=ALU.mult),
                         r=[brl] + (alias_all if ffc == 0 else []), w=[bact[ffc]] + (alias_all if ffc == 0 else []))
            if stop == 'p2f':
                raise _Stop()
            if st < 3:
                sup.ensure((st + 1) * 16)
            pre = []
            for i in range(8):
                wt, bw = sdn.get(st * 8 + i)
                for t in range(4):
                    for hf in range(2):
                        ps, bp = psr.next()
                        for f in range(4):
                            P.op("pe", lambda e: e.matmul(
                                ps[:, :], lhsT=actT[:, i * 4 + f, t * 128:(t + 1) * 128], rhs=wt[:, f, hf * 512:(hf + 1) * 512],
                                start=(f == 0), stop=(f == 3)), r=[bact[i * 4 + f], bw], w=[bp], inc=(f == 3))
                        P.op("dve", lambda e: e.tensor_tensor(
                            out=xh[:, t, hf * 512:(hf + 1) * 512], in0=ps[:, :], in1=xh[:, t, hf * 512:(hf + 1) * 512],
                            op=ALU.add), r=[bp, bxh[t]], w=[bxh[t]])
                if st < 3:
                    if i == 0:
                        P.op("dve", lambda e: e.memset(dummy_t[:, 1:2], 0.0), r=bact[0:4], w=xnr.bufs)
                        load_rope(rc, rs, pos_base + (st + 1) * 512, 512)
                    if i in (0, 2, 4, 6):
                        tn = i // 2
                        sdma(xstage[:, :], x_src((st + 1) * 4 + tn), w=bxs)
                        xn, bxn = xnr.next()
                        rms_scale(xstage[:, :], bxs, 128, g1b, xn[:, :], bxn, xn[:, :], bxn)
                        pre.append((tn, xn, bxn))
                    if i in (1, 3, 5, 7):
                        (t2, xn2, bxn2) = pre.pop(0)
                        transposes(xn2, bxn2, 128, xnT[:, :, t2 * 128:(t2 + 1) * 128], bxnT[t2], "act")
            P.op("dve", lambda e: e.memset(dummy_t[:, 0:1], 0.0), r=bact, w=alias_all)
            if stop == 'p2g':
                raise _Stop()
            if st == 3:
                final_norm(3)

    def run_all():
        for s_ in range(2):
            tiles = [(meta[:, :], NMETA, 0, 18, (c_rc_p, c_rs_p, 0))]
            for j in range(16):
                tiles.append((xp[s_, j * 128:(j + 1) * 128, :], 128, NMETA + j * 128, j, (c_rc_p, c_rs_p, NMETA + j * 128)))
            if stop == "setup":
                return
            TBt, bTBt = phase1(tiles, (NA_P, NBO_P, NB_P, tbp_bf))
            if stop == "p1":
                return
            fourier(LP, NA_P, NB_P, NBO_P, fa_p, TBt, bTBt, 128)
            if stop == "f":
                return
            phase2(lambda t, s_=s_: xp[s_, t * 128:(t + 1) * 128, :], lambda t, s_=s_: yp[s_, t * 128:(t + 1) * 128, :],
                   c_rc_p, c_rs_p, NMETA, 0, 16, False)
            if stop == "p2":
                return
        tiles = [(meta[:, :], NMETA, 0, 18, (c_rc_p, c_rs_p, 0))]
        for j in range(64):
            tiles.append((xs[j * 128:(j + 1) * 128, :], 128, NMETA + j * 128, None, None))
        for j in range(18):
            tiles.append((xso[j * 128:(j + 1) * 128, :], 128, None, j, (c_rc_s, c_rs_s, j * 128)))
        TBt, bTBt = phase1(tiles, (NA_S, NBO_S, NB_S, tbs_bf))
        if stop == "sp1":
            return
        fourier(LS, NA_S, NB_S, NBO_S, fa_s, TBt, bTBt, 64)
        if stop == "sf":
            return
        phase2(lambda t: xso[HALO + t * 128:HALO + (t + 1) * 128, :], lambda t: ys[t * 128:(t + 1) * 128, :],
               c_rc_s, c_rs_s, HALO, 1, 18, True)

    try:
        run_all()
    except _Stop:
        pass
    P.barrier(final=True)

    from contextlib import ExitStack
    with ExitStack() as es:
        sems = {}
        for name in list(Prog.ENGS) + list(P.dcnt.keys()):
            sems[name] = es.enter_context(nc.semaphore("s_" + name))
        block = es.enter_context(nc.Block())

        @block.tensor
        def _(e):
            P.emit("pe", e, sems)

        @block.scalar
        def _(e):
            P.emit("act", e, sems)

        @block.vector
        def _(e):
            P.emit("dve", e, sems)

        @block.gpsimd
        def _(e):
            P.emit("pool", e, sems)

        @block.sync
        def _(e):
            P.emit("sp", e, sems)
    return nc


_NC = None


def kernel(x_prompt, x_sample, meta_tokens, norm_mix_g, w_in, b_gate, attn_sink, w_attn_out, w_fourier, w_out,
           norm_mlp_g, w_mlp_up, w_mlp_down, norm_final_g):
    global _NC
    if _NC is None:
        _NC = build_program()
    nc = _NC
    f = lambda a: np.ascontiguousarray(np.asarray(a, dtype=np.float32))
    x_prompt, x_sample = f(x_prompt), f(x_sample)
    shared = {
        "meta": f(meta_tokens), "w_in": f(w_in[0]), "b_gate": f(np.asarray(b_gate[0]).reshape(16, 128).T),
        "sink": f(np.asarray(attn_sink[0]).reshape(1, 8)), "w_ao": f(w_attn_out[0]), "w_f": f(w_fourier[0]),
        "w_out": f(w_out[0]), "w_up": f(w_mlp_up[0]), "w_dn": f(w_mlp_down[0]),
        "g1": f(np.asarray(norm_mix_g[0]).reshape(1, D)), "g2": f(np.asarray(norm_mlp_g[0]).reshape(1, D)),
        "g3": f(np.asarray(norm_final_g).reshape(1, D)),
    }
    in_maps = []
    for c in range(8):
        r, sq = c % 4, c // 4
        pad = np.zeros((OWN + 2 * HALO, D), np.float32)
        lo, hi = OWN * r - HALO, OWN * r + OWN + HALO
        slo, shi = max(lo, 0), min(hi, DSEQ)
        pad[slo - lo:shi - lo] = x_sample[sq, slo:shi]
        m = dict(shared)
        m["xp"] = np.ascontiguousarray(x_prompt[2 * c:2 * c + 2])
        m["xs"] = np.ascontiguousarray(x_sample[sq])
        m["xso"] = pad
        m.update(_consts(c))
        in_maps.append(m)
    res = run_bass_kernel_spmd(nc, in_maps, core_ids=list(range(8)))
    y_prompt = np.empty((16, SEQ, D), np.float32)
    y_sample = np.empty((2, DSEQ, D), np.float32)
    for c in range(8):
        r, sq = c % 4, c // 4
        y_prompt[2 * c:2 * c + 2] = np.asarray(res.results[c]["yp"]).reshape(2, SEQ, D)
        y_sample[sq, OWN * r:OWN * (r + 1)] = np.asarray(res.results[c]["ys"]).reshape(OWN, D)
    return (y_prompt, y_sample)
```
